# Optimizing a Trainium2 kernel written in Bass

```python
import math
import jax
import jax.numpy as jnp
from jax import lax
import numpy as np

D_MODEL = 1024
BATCH = 16
SEQ = 2048
DEPTH = 4
DEC_BATCH = 8
DEC_SEQ = 64
PAST_LEN = 4096

CHUNK = 64
EPS = 1e-6
H_A = 4
DK_A = 128
DV_A = 128
CONV_A = 4
H_B = 4
DK_B = 64
DV_B = 128
GATE_RANK = 16
GATE_NORM = 16.0
N_MEM = 256
MEM_HEADS = 4
MEM_HD = D_MODEL // MEM_HEADS
D_FF = 2816
CONV_F = 3
QA = H_A * DK_A
VA = H_A * DV_A
QKV_A = 2 * QA + VA
KB = H_B * DK_B
VB = H_B * DV_B
IN_SIZES = (QKV_A, H_A, H_A, VA, KB, KB, VB, GATE_RANK, VB, D_MODEL, D_MODEL)
N_IN = sum(IN_SIZES)

kernel_name = 'hybrid_gdn_gla_stream_step'


def _rmsnorm(x, g):
    xf = x.astype(jnp.float32)
    y = xf * lax.rsqrt(jnp.mean(xf * xf, axis=-1, keepdims=True) + EPS)
    return (y * g.astype(jnp.float32)).astype(x.dtype)


def _l2norm(x):
    return x * lax.rsqrt(jnp.sum(x * x, axis=-1, keepdims=True) + EPS)


def _split_cols(x, sizes):
    out, start = [], 0
    for s in sizes:
        out.append(x[..., start:start + s])
        start += s
    return out


def _causal_dwconv(x, w, prev):
    width, L = w.shape[0], x.shape[1]
    xp = jnp.concatenate([prev.astype(x.dtype), x], axis=1)
    y = xp[:, 0:L] * w[0]
    for i in range(1, width):
        y = y + xp[:, i:i + L] * w[i]
    return y, xp[:, L:]


def _pad_seq(x, lp):
    pad = [(0, 0)] * x.ndim
    pad[1] = (0, lp - x.shape[1])
    return jnp.pad(x, pad)


def _to_chunks(x):
    b, lp, h = x.shape[:3]
    x = x.reshape((b, lp // CHUNK, CHUNK, h) + x.shape[3:])
    return jnp.moveaxis(x, (1, 3), (0, 2))


def _from_chunks(x):
    x = jnp.moveaxis(x, (0, 2), (1, 3))
    return x.reshape((x.shape[0], x.shape[1] * x.shape[2]) + x.shape[3:])


def _gated_delta_chunked(q, k, v, g, beta, s0):
    L, dk = q.shape[1], q.shape[-1]
    lp = -(-L // CHUNK) * CHUNK
    valid = (jnp.arange(lp) < L)[None, :, None]
    qc = _to_chunks(_pad_seq(q, lp))
    kc = _to_chunks(_pad_seq(k, lp))
    vc = _to_chunks(_pad_seq(v, lp))
    gc = _to_chunks(jnp.where(valid, _pad_seq(g, lp), 0.0))
    bc = _to_chunks(jnp.where(valid, _pad_seq(beta, lp), 0.0))
    G = jnp.cumsum(gc, axis=-1)
    tri = jnp.tril(jnp.ones((CHUNK, CHUNK), dtype=bool))
    strict = tri & ~jnp.eye(CHUNK, dtype=bool)
    decay = jnp.exp(jnp.where(tri, G[..., :, None] - G[..., None, :], -jnp.inf))
    kb = kc * bc[..., None]
    a_kk = jnp.where(strict, jnp.einsum('nbhcd,nbhsd->nbhcs', kb, kc) * decay, 0.0)
    eye = jnp.eye(CHUNK, dtype=q.dtype)
    rhs = jnp.concatenate([kb * jnp.exp(G)[..., None], vc * bc[..., None]], axis=-1)
    wu = lax.linalg.triangular_solve(eye + a_kk, rhs, left_side=True, lower=True, unit_diagonal=True)
    wc, uc = wu[..., :dk], wu[..., dk:]
    a_qk = jnp.einsum('nbhcd,nbhsd->nbhcs', qc, kc) * decay
    qg = qc * jnp.exp(G)[..., None]
    kg = kc * jnp.exp(G[..., -1:] - G)[..., None]
    dl = jnp.exp(G[..., -1])

    def step(s, inp):
        qg_n, kg_n, w_n, u_n, aqk_n, dl_n = inp
        v_new = u_n - jnp.einsum('bhcd,bhdv->bhcv', w_n, s)
        o = jnp.einsum('bhcd,bhdv->bhcv', qg_n, s) + jnp.einsum('bhcs,bhsv->bhcv', aqk_n, v_new)
        s = s * dl_n[..., None, None] + jnp.einsum('bhcd,bhcv->bhdv', kg_n, v_new)
        return s, o

    s, o = lax.scan(step, s0, (qg, kg, wc, uc, a_qk, dl))
    return _from_chunks(o)[:, :L], s


def _gla_chunked(q, k, v, log_a, s0):
    L = q.shape[1]
    lp = -(-L // CHUNK) * CHUNK
    valid = (jnp.arange(lp) < L)[None, :, None, None]
    qc = _to_chunks(_pad_seq(q, lp))
    kc = _to_chunks(_pad_seq(k, lp))
    vc = _to_chunks(_pad_seq(v, lp))
    lc = _to_chunks(jnp.where(valid, _pad_seq(log_a, lp), 0.0))
    G = jnp.cumsum(lc, axis=-2)
    tri = jnp.tril(jnp.ones((CHUNK, CHUNK), dtype=bool))
    qg = qc * jnp.exp(G)
    a_qk = jnp.where(tri, jnp.einsum('nbhcd,nbhsd->nbhcs', qg, kc * jnp.exp(-G)), 0.0)
    kg = kc * jnp.exp(G[..., -1:, :] - G)
    dl = jnp.exp(G[..., -1, :])

    def step(s, inp):
        qg_n, kg_n, v_n, aqk_n, dl_n = inp
        o = jnp.einsum('bhcd,bhdv->bhcv', qg_n, s) + jnp.einsum('bhcs,bhsv->bhcv', aqk_n, v_n)
        s = s * dl_n[..., None] + jnp.einsum('bhcd,bhcv->bhdv', kg_n, v_n)
        return s, o

    s, o = lax.scan(step, s0, (qg, kg, vc, a_qk, dl))
    return _from_chunks(o)[:, :L], s


def _mixer(h, p, l, conv_prev, s_a, s_b):
    f32 = jnp.float32
    B, L, _ = h.shape
    proj = h @ p['w_in'][l]
    qkv, b_raw, a_raw, z_a, q_b, k_b, v_b, g_lr, z_b, gt_a, gt_b = _split_cols(proj, IN_SIZES)
    qkv, conv_new = _causal_dwconv(qkv, p['conv_a_w'][l], conv_prev)
    qkv = jax.nn.silu(qkv.astype(f32))
    q_a = _l2norm(qkv[..., :QA].reshape(B, L, H_A, DK_A)) * (DK_A ** -0.5)
    k_a = _l2norm(qkv[..., QA:2 * QA].reshape(B, L, H_A, DK_A))
    v_a = qkv[..., 2 * QA:].reshape(B, L, H_A, DV_A)
    beta = jax.nn.sigmoid(b_raw.astype(f32))
    g_a = -jnp.exp(p['a_log'][l].astype(f32)) * jax.nn.softplus(a_raw.astype(f32) + p['dt_bias'][l].astype(f32))
    o_a, s_a_new = _gated_delta_chunked(q_a, k_a, v_a, g_a, beta, s_a.astype(f32))
    o_a = _rmsnorm(o_a, p['onorm_a'][l]) * jax.nn.silu(z_a.astype(f32).reshape(B, L, H_A, DV_A))
    q_b = q_b.astype(f32).reshape(B, L, H_B, DK_B) * (DK_B ** -0.5)
    k_b = k_b.astype(f32).reshape(B, L, H_B, DK_B)
    v_b = v_b.astype(f32).reshape(B, L, H_B, DV_B)
    log_a = jax.nn.log_sigmoid((g_lr @ p['w_gate_b2'][l] + p['b_gate_b'][l]).astype(f32)).reshape(B, L, H_B, DK_B) / GATE_NORM
    o_b, s_b_new = _gla_chunked(q_b, k_b, v_b, log_a, s_b.astype(f32))
    o_b = _rmsnorm(o_b, p['onorm_b'][l]) * jax.nn.silu(z_b.astype(f32).reshape(B, L, H_B, DV_B))
    y_a = o_a.reshape(B, L, VA).astype(h.dtype) @ p['w_out_a'][l]
    y_b = o_b.reshape(B, L, VB).astype(h.dtype) @ p['w_out_b'][l]
    merged = jax.nn.sigmoid(gt_a) * y_a + jax.nn.sigmoid(gt_b) * y_b
    return merged @ p['w_o'][l], conv_new, s_a_new, s_b_new


def _mem_kv(mem, g, w_k, w_v):
    B, M, _ = mem.shape
    m = _rmsnorm(mem, g)
    return (m @ w_k).reshape(B, M, MEM_HEADS, MEM_HD), (m @ w_v).reshape(B, M, MEM_HEADS, MEM_HD)


def _cross_attn(h, k, v, w_q, w_o):
    B, L, _ = h.shape
    q = (h @ w_q).reshape(B, L, MEM_HEADS, MEM_HD).astype(jnp.float32)
    s = jnp.einsum('blhd,bmhd->bhlm', q, k.astype(jnp.float32)) * (MEM_HD ** -0.5)
    a = jax.nn.softmax(s, axis=-1)
    o = jnp.einsum('bhlm,bmhd->blhd', a, v.astype(jnp.float32)).reshape(B, L, D_MODEL)
    return o.astype(h.dtype) @ w_o


def _conv_ffn(h, w_up, conv_w, conv_b, w_down, prev):
    u, new = _causal_dwconv(h @ w_up, conv_w, prev)
    u = u + conv_b
    gate, val = u[..., :D_FF], u[..., D_FF:]
    return (jax.nn.silu(gate) * val) @ w_down, new


def _trunk(x, mem_k, mem_v, conv_a_prev, s_gdn, s_gla, ffn_prev, p):
    conv_out, sa_out, sb_out, ffn_out = [], [], [], []
    for l in range(DEPTH):
        y, c_new, sa, sb = _mixer(_rmsnorm(x, p['norm_mix'][l]), p, l, conv_a_prev[l], s_gdn[l], s_gla[l])
        x = x + y.astype(x.dtype)
        x = x + _cross_attn(_rmsnorm(x, p['norm_mem'][l]), mem_k[l], mem_v[l], p['w_mq'][l], p['w_mo'][l]).astype(x.dtype)
        f, f_new = _conv_ffn(_rmsnorm(x, p['norm_ffn'][l]), p['w_up'][l], p['conv_f_w'][l], p['conv_f_b'][l], p['w_down'][l], ffn_prev[l])
        x = x + f.astype(x.dtype)
        conv_out.append(c_new)
        sa_out.append(sa)
        sb_out.append(sb)
        ffn_out.append(f_new)
    return (_rmsnorm(x, p['norm_final']), jnp.stack(conv_out), jnp.stack(sa_out), jnp.stack(sb_out), jnp.stack(ffn_out))


def setup_inputs(seed: int = 0) -> dict:
    key = jax.random.key(seed)
    ks = iter(jax.random.split(key, 48))

    def nrm(shape, scale):
        return jax.random.normal(next(ks), shape, jnp.float32) * scale

    def gain(shape):
        return 1.0 + nrm(shape, 0.02)

    a_log = jnp.log(jax.random.uniform(next(ks), (DEPTH, H_A), jnp.float32, 1.0, 16.0))
    dt = jnp.exp(jax.random.uniform(next(ks), (DEPTH, H_A), jnp.float32, math.log(1e-3), math.log(1e-1)))
    dt_bias = dt + jnp.log(-jnp.expm1(-dt))
    return {
        'x_prompt': nrm((BATCH, SEQ, D_MODEL), 1.0),
        'x_sample': nrm((DEC_BATCH, DEC_SEQ, D_MODEL), 1.0),
        'state_gdn': nrm((DEPTH, DEC_BATCH, H_A, DK_A, DV_A), 0.3),
        'state_gdn_conv': nrm((DEPTH, DEC_BATCH, CONV_A - 1, QKV_A), 1.0),
        'state_gla': nrm((DEPTH, DEC_BATCH, H_B, DK_B, DV_B), 0.5),
        'state_ffn_conv': nrm((DEPTH, DEC_BATCH, CONV_F - 1, 2 * D_FF), 1.0),
        'cache_mem_k': nrm((DEPTH, DEC_BATCH, N_MEM, MEM_HEADS, MEM_HD), 1.0),
        'cache_mem_v': nrm((DEPTH, DEC_BATCH, N_MEM, MEM_HEADS, MEM_HD), 1.0),
        'mem_prompt': nrm((BATCH, N_MEM, D_MODEL), 1.0),
        'norm_mix': gain((DEPTH, D_MODEL)),
        'w_in': nrm((DEPTH, D_MODEL, N_IN), D_MODEL ** -0.5),
        'conv_a_w': nrm((DEPTH, CONV_A, QKV_A), CONV_A ** -0.5),
        'a_log': a_log,
        'dt_bias': dt_bias,
        'onorm_a': gain((DEPTH, DV_A)),
        'w_gate_b2': nrm((DEPTH, GATE_RANK, KB), GATE_RANK ** -0.5),
        'b_gate_b': nrm((DEPTH, KB), 0.1),
        'onorm_b': gain((DEPTH, DV_B)),
        'w_out_a': nrm((DEPTH, VA, D_MODEL), VA ** -0.5),
        'w_out_b': nrm((DEPTH, VB, D_MODEL), VB ** -0.5),
        'w_o': nrm((DEPTH, D_MODEL, D_MODEL), D_MODEL ** -0.5),
        'norm_mem': gain((DEPTH, D_MODEL)),
        'norm_memkv': gain((DEPTH, D_MODEL)),
        'w_mq': nrm((DEPTH, D_MODEL, D_MODEL), D_MODEL ** -0.5),
        'w_mk': nrm((DEPTH, D_MODEL, D_MODEL), D_MODEL ** -0.5),
        'w_mv': nrm((DEPTH, D_MODEL, D_MODEL), D_MODEL ** -0.5),
        'w_mo': nrm((DEPTH, D_MODEL, D_MODEL), D_MODEL ** -0.5),
        'norm_ffn': gain((DEPTH, D_MODEL)),
        'w_up': nrm((DEPTH, D_MODEL, 2 * D_FF), D_MODEL ** -0.5),
        'conv_f_w': nrm((DEPTH, CONV_F, 2 * D_FF), CONV_F ** -0.5),
        'conv_f_b': nrm((DEPTH, 2 * D_FF), 0.02),
        'w_down': nrm((DEPTH, D_FF, D_MODEL), D_FF ** -0.5),
        'norm_final': gain((D_MODEL,)),
    }


def reference(x_prompt, x_sample, state_gdn, state_gdn_conv, state_gla, state_ffn_conv, cache_mem_k, cache_mem_v, mem_prompt, norm_mix, w_in, conv_a_w, a_log, dt_bias, onorm_a, w_gate_b2, b_gate_b, onorm_b, w_out_a, w_out_b, w_o, norm_mem, norm_memkv, w_mq, w_mk, w_mv, w_mo, norm_ffn, w_up, conv_f_w, conv_f_b, w_down, norm_final):
    p = dict(norm_mix=norm_mix, w_in=w_in, conv_a_w=conv_a_w, a_log=a_log, dt_bias=dt_bias, onorm_a=onorm_a, w_gate_b2=w_gate_b2, b_gate_b=b_gate_b, onorm_b=onorm_b, w_out_a=w_out_a, w_out_b=w_out_b, w_o=w_o, norm_mem=norm_mem, w_mq=w_mq, w_mo=w_mo, norm_ffn=norm_ffn, w_up=w_up, conv_f_w=conv_f_w, conv_f_b=conv_f_b, w_down=w_down, norm_final=norm_final)
    mk, mv = [], []
    for l in range(DEPTH):
        k_l, v_l = _mem_kv(mem_prompt, norm_memkv[l], w_mk[l], w_mv[l])
        mk.append(k_l)
        mv.append(v_l)
    mem_k_p = jnp.stack(mk)
    mem_v_p = jnp.stack(mv)
    B = x_prompt.shape[0]
    zc = jnp.zeros((DEPTH, B, CONV_A - 1, QKV_A), x_prompt.dtype)
    za = jnp.zeros((DEPTH, B, H_A, DK_A, DV_A), jnp.float32)
    zb = jnp.zeros((DEPTH, B, H_B, DK_B, DV_B), jnp.float32)
    zf = jnp.zeros((DEPTH, B, CONV_F - 1, 2 * D_FF), x_prompt.dtype)
    y_prompt, gdn_conv_p, gdn_p, gla_p, ffn_conv_p = _trunk(x_prompt, mem_k_p, mem_v_p, zc, za, zb, zf, p)
    y_sample, gdn_conv_s, gdn_s, gla_s, ffn_conv_s = _trunk(x_sample, cache_mem_k, cache_mem_v, state_gdn_conv, state_gdn, state_gla, state_ffn_conv, p)
    return (y_prompt, y_sample, gdn_p, gdn_conv_p, gla_p, ffn_conv_p, mem_k_p, mem_v_p, gdn_s, gdn_conv_s, gla_s, ffn_conv_s)
```

```python
import bisect
import contextlib
import numpy as np
import concourse.bass as bass
import concourse.mybir as mybir
from concourse.bass_utils import run_bass_kernel_spmd

F32 = mybir.dt.float32
BF16 = mybir.dt.bfloat16
AF = mybir.ActivationFunctionType
ALU = mybir.AluOpType

D = 1024
KC = 8
L_FULL = 4
SEQ_FULL = 2048
NMEM = 256
DFF = 2816
NIN = 5656
EPS = 1e-6
NU = 38
UNIT = 4096
NSLOT = 3
SEMLIM = 30000
BIG = 30000.0

PC_NMIX, PC_NMEM, PC_NFFN, PC_NKV, PC_CONVA, PC_ONA, PC_ONB, PC_CFW, PC_CFB, PC_NFIN = 0, 8, 16, 24, 32, 80, 81, 82, 214, 258
NPC = 266
NPR = 264


def _esize(dt):
    return 2 if dt == BF16 else 4


class Op:
    __slots__ = ("eng", "fn", "deps", "sig", "sem", "val", "dma", "n")

    def __init__(self, eng, fn, dma):
        self.eng, self.fn, self.dma = eng, fn, dma
        self.deps = {}
        self.sig = False
        self.sem = None
        self.val = 0


class Prog:
    ENGS = ("pe", "act", "dve", "pool", "sp")

    def __init__(self):
        self.ops = {e: [] for e in self.ENGS}
        self.segs = {}
        self.nops = 0

    @staticmethod
    def region(ap):
        t = ap.tensor
        es = _esize(ap.dtype)
        dims = ap.ap
        if type(t).__name__ == "DRamTensorHandle":
            lo = ap.offset
            ext = sum((c - 1) * abs(s) for s, c in dims) + 1
        elif type(t).__name__ == "PSumTensorHandle":
            return t.name, 0, 2048
        else:
            pstep = dims[0][0]
            lo = ap.offset % pstep if pstep else ap.offset
            ext = sum((c - 1) * abs(s) for s, c in dims[1:]) + 1
        return t.name, lo * es, (lo + ext) * es

    def _touch(self, op, ap, write):
        name, lo, hi = self.region(ap)
        st = self.segs.setdefault(name, [[], []])
        starts, segs = st
        i = bisect.bisect_right(starts, lo) - 1
        if i >= 0 and segs[i][1] > lo and segs[i][0] < lo:
            s = segs[i]
            ns = [lo, s[1], s[2], dict(s[3])]
            s[1] = lo
            segs.insert(i + 1, ns)
            starts.insert(i + 1, lo)
        i = bisect.bisect_left(starts, lo)
        cur = lo
        out = []
        while cur < hi:
            if i < len(segs) and segs[i][0] == cur:
                s = segs[i]
                if s[1] > hi:
                    ns = [hi, s[1], s[2], dict(s[3])]
                    s[1] = hi
                    segs.insert(i + 1, ns)
                    starts.insert(i + 1, hi)
                out.append(s)
                cur = s[1]
                i += 1
            else:
                nxt = segs[i][0] if i < len(segs) else hi
                nxt = min(nxt, hi)
                s = [cur, nxt, None, {}]
                segs.insert(i, s)
                starts.insert(i, cur)
                out.append(s)
                cur = nxt
                i += 1
        for s in out:
            if s[2] is not None and s[2] is not op:
                op.deps[id(s[2])] = s[2]
            if write:
                for r in s[3].values():
                    if r is not op:
                        op.deps[id(r)] = r
        if write:
            first = out[0]
            i0 = bisect.bisect_left(starts, first[0])
            del segs[i0:i0 + len(out)]
            del starts[i0:i0 + len(out)]
            segs.insert(i0, [lo, hi, op, {}])
            starts.insert(i0, lo)
        else:
            for s in out:
                key = op.eng if not op.dma else ("dma", id(op))
                s[3][key] = op

    def add(self, eng, fn, reads=(), writes=(), dma=False):
        op = Op(eng, fn, dma)
        op.n = self.nops
        self.nops += 1
        for ap in reads:
            self._touch(op, ap, type(ap.tensor).__name__ == "PSumTensorHandle")
        for ap in writes:
            self._touch(op, ap, True)
        self.ops[eng].append(op)
        return op

    def emit(self, nc, stack):
        for e in self.ENGS:
            for op in self.ops[e]:
                for d in op.deps.values():
                    if d.eng == "pe" and op.eng == "pe" and not d.dma:
                        continue
                    d.sig = True
        final_waits = {}
        for e in self.ENGS:
            cnt = 0
            sems = []
            for op in self.ops[e]:
                if op.dma or not op.sig:
                    continue
                k = cnt // SEMLIM
                if k >= len(sems):
                    sems.append(stack.enter_context(nc.semaphore("s_%s_%d" % (e, k))))
                op.sem = sems[k]
                op.val = cnt % SEMLIM + 1
                cnt += 1
        dma_prev = {}
        for e in self.ENGS:
            pool = None
            i = 0
            for op in self.ops[e]:
                if not op.dma:
                    continue
                if pool is None:
                    pool = [[stack.enter_context(nc.semaphore("d_%s_%d" % (e, j))), 0, None] for j in range(12)]
                slot = pool[i % len(pool)]
                i += 1
                slot[1] += 16
                op.sem = slot[0]
                op.val = slot[1]
                op.sig = True
                if slot[2] is not None:
                    op.deps[id(slot[2])] = slot[2]
                slot[2] = op
                final_waits[id(op.sem)] = (op.sem, op.val)
        engobj = {"pe": "tensor", "act": "scalar", "dve": "vector", "pool": "gpsimd", "sp": "sync"}
        prog = self

        def run(ename, e):
            seen = {}
            for op in prog.ops[ename]:
                for d in op.deps.values():
                    if d.eng == "pe" and ename == "pe" and not d.dma:
                        continue
                    k = id(d.sem)
                    if seen.get(k, 0) >= d.val:
                        continue
                    seen[k] = d.val
                    e.wait_ge(d.sem, d.val)
                ins = op.fn(e)
                if op.sig:
                    ins.then_inc(op.sem, 16 if op.dma else 1)
            if ename == "sp":
                for sem, val in final_waits.values():
                    e.wait_ge(sem, val)

        with nc.Block() as block:
            @block.sync
            def _(e):
                run("sp", e)

            @block.tensor
            def _(e):
                run("pe", e)

            @block.scalar
            def _(e):
                run("act", e)

            @block.vector
            def _(e):
                run("dve", e)

            @block.gpsimd
            def _(e):
                run("pool", e)


class StopBuild(Exception):
    pass


class Cfg:
    def __init__(self, NP=2, SEQ=SEQ_FULL, L=L_FULL, sample=True, TT=512, stop=None):
        self.NP, self.SEQ, self.L, self.sample, self.TT, self.stop = NP, SEQ, L, sample, TT, stop


def build(cfg):
    NP, SEQ, L = cfg.NP, cfg.SEQ, cfg.L
    TT = min(cfg.TT, SEQ)
    nc = bass.Bass("TRN2", target_bir_lowering=False)
    P = Prog()
    ES = contextlib.ExitStack()

    def din(name, shape, dt=F32):
        return nc.dram_tensor(name, list(shape), dt, kind="ExternalInput").ap()

    def dout(name, shape):
        return nc.dram_tensor(name, list(shape), F32, kind="ExternalOutput").ap()

    def dscr(name, shape, dt=BF16):
        return nc.dram_tensor(name, list(shape), dt).ap()

    NS = 1 if cfg.sample else 0
    xp = din("xp", [NP, SEQ, D])
    memp = din("memp", [NP, NMEM, D])
    wst = din("wst", [L, NU, 128, UNIT])
    wkv = din("wkv", [L, 4, 128, UNIT])
    wsm = din("wsm", [L, 128, KC * 24])
    w2 = din("w2", [16, L * 256])
    pcol = din("pcol", [128, L * NPC])
    prow = din("prow", [1, L * NPR])
    cst = din("cst", [128, 640])
    if NS:
        xs = din("xs", [64, D])
        sgdn = din("sgdn", [L, 4, 128, 128])
        sgdc = din("sgdc", [L, 128, 36])
        sgla = din("sgla", [L, 4, 64, 128])
        sffc = din("sffc", [L, 128, 88])
        cmk = din("cmk", [L, NMEM, D])
        cmv = din("cmv", [L, NMEM, D])
    yp = dout("yp", [NP, SEQ, D])
    o_gdn = dout("o_gdn", [L, NP + NS, 4, 128, 128])
    o_gdc = dout("o_gdc", [L, NP + NS, 3, 1536])
    o_gla = dout("o_gla", [L, NP + NS, 4, 64, 128])
    o_ffc = dout("o_ffc", [L, NP + NS, 2, 2 * DFF])
    o_mk = dout("o_mk", [L, NP, NMEM, D])
    o_mv = dout("o_mv", [L, NP, NMEM, D])
    if NS:
        ys = dout("ys", [64, D])
    wbf = dscr("wbf", [L, NU, 128, UNIT])
    wkvbf = dscr("wkvbf", [L, 4, 128, UNIT])
    kts = dscr("kts", [L, 128, KC * NMEM])
    vsc = dscr("vsc", [L, 128, 2 * D])

    def sb(name, shape, dt=F32):
        return ES.enter_context(nc.sbuf_tensor(name, list(shape), dt))

    xT = sb("xT", [128, KC, TT])
    xn = sb("xn", [128, KC, TT], BF16)
    wring = [sb("wr%d" % i, [128, UNIT], BF16) for i in range(NSLOT)]
    Sg = sb("Sg", [128, L, 4, 128])
    Sgb = sb("Sgb", [128, L, 4, 128], BF16)
    Sl = sb("Sl", [128, L, 2, 128])
    Slb = sb("Slb", [128, L, 2, 128], BF16)
    car_a = sb("car_a", [128, L, 12, 3])
    car_f = sb("car_f", [128, L, 44, 2])
    cs = sb("cs", [128, 640])
    identf = cs[:, 0:128]
    triU = cs[:, 128:256]
    NEGT = cs[:, 256:384]
    POSL = cs[:, 384:512]
    onesf = cs[:, 512:640]
    csb = sb("csb", [128, 384], BF16)
    identb = csb[:, 0:128]
    triUb = csb[:, 128:256]
    onesb = csb[:, 256:384]
    pc = sb("pc", [128, L * NPC])
    pr = sb("pr", [128, L * NPR])
    nega = sb("nega", [128, L * 4])
    wsmb = sb("wsmb", [128, L, KC * 24], BF16)
    w2b = sb("w2b", [16, L * 256], BF16)
    AR_BYTES = 124 * 1024
    AR = sb("AR", [128, AR_BYTES // 4])
    banks = [ES.enter_context(nc.psum_tensor("ps%d" % i, [128, 512], F32)) for i in range(8)]
    bank_i = [0]

    def bank():
        b = banks[bank_i[0] % 8]
        bank_i[0] += 1
        return b

    ar_top = [0]
    pf_live = [False]

    def aalloc(shape, dt=F32):
        n = 1
        for s_ in shape[1:]:
            n *= s_
        nb = n * _esize(dt)
        nb = (nb + 63) // 64 * 64
        off = ar_top[0]
        ar_top[0] += nb
        assert ar_top[0] <= AR_BYTES, ("arena overflow", ar_top[0])
        assert not (pf_live[0] and ar_top[0] > AR_BYTES - 16384), ("prefetch region clobbered", ar_top[0])
        v = AR[:, off // 4:(off + nb) // 4]
        if dt == BF16:
            v = v.bitcast(BF16)
        v = v[:, 0:n]
        if len(shape) == 3:
            v = v.rearrange("p (a b) -> p a b", b=shape[2])
        elif len(shape) == 4:
            v = v.rearrange("p (a b c) -> p a b c", b=shape[2], c=shape[3])
        return v

    def ckpt(name):
        if cfg.stop == name:
            raise StopBuild()

    def isap(a):
        return not isinstance(a, (int, float)) and a is not None

    def tt(eng, out, in0, in1, op):
        P.add(eng, lambda e: e.tensor_tensor(out=out, in0=in0, in1=in1, op=op), [in0, in1], [out])

    def ts(eng, out, in0, s1, s2=None, op0=ALU.mult, op1=None):
        rd = [in0] + [s for s in (s1, s2) if isap(s)]
        if s2 is None:
            P.add(eng, lambda e: e.tensor_scalar(out=out, in0=in0, scalar1=s1, scalar2=None, op0=op0), rd, [out])
        else:
            P.add(eng, lambda e: e.tensor_scalar(out=out, in0=in0, scalar1=s1, scalar2=s2, op0=op0, op1=op1), rd, [out])

    def stt(eng, out, in0, scalar, in1, op0, op1):
        rd = [in0, in1] + ([scalar] if isap(scalar) else [])
        eng = "dve"
        P.add(eng, lambda e: e.scalar_tensor_tensor(out=out, in0=in0, scalar=scalar, in1=in1, op0=op0, op1=op1), rd, [out])

    def cp(eng, out, in_):
        if eng == "act":
            P.add("act", lambda e: e.activation(out=out, in_=in_, func=AF.Copy), [in_], [out])
        else:
            P.add(eng, lambda e: e.tensor_copy(out=out, in_=in_), [in_], [out])

    def act(out, in_, func, bias=None, scale=None, accum=None):
        rd = [in_] + [s for s in (bias, scale) if isap(s)]
        wr = [out] + ([accum] if accum is not None else [])
        kw = {}
        if bias is not None:
            kw["bias"] = bias
        if scale is not None:
            kw["scale"] = scale
        if accum is not None:
            kw["accum_out"] = accum
        P.add("act", lambda e: e.activation(out=out, in_=in_, func=func, **kw), rd, wr)

    def mm(out, lhsT, rhs, start=True, stop=True):
        P.add("pe", lambda e: e.matmul(out, lhsT, rhs, start=start, stop=stop), [lhsT, rhs], [out])

    def tr(out, in_, ident):
        P.add("pe", lambda e: e.transpose(out, in_, ident), [in_, ident], [out])

    def dma(q, out, in_, **kw):
        P.add(q, lambda e: e.dma_start(out=out, in_=in_, **kw), [in_], [out], dma=True)

    def memset(eng, ap, v):
        P.add(eng, lambda e: e.memset(ap, v), [], [ap])

    def recip(out, in_):
        P.add("dve", lambda e: e.reciprocal(out=out, in_=in_), [in_], [out])

    def rmax(out, in_):
        P.add("dve", lambda e: e.reduce_max(out=out, in_=in_, axis=mybir.AxisListType.X), [in_], [out])

    dma("sp", cs[:, :], cst[:, :])
    dma("sp", pc[:, :], pcol[:, :])
    dma("sp", pr[:, :], prow[0:1, :].partition_broadcast(128))
    cp("dve", csb[:, 0:256], cs[:, 0:256])
    cp("dve", csb[:, 256:384], cs[:, 512:640])
    dma("pool", w2b[:, :], w2[:, :])
    for l in range(L):
        dma("pool", wsmb[:, l, :], wsm[l, :, :])
        act(nega[:, l * 4:l * 4 + 4], pr[:, l * NPR:l * NPR + 4], AF.Exp)
    ts("dve", nega[:, :], nega[:, :], -1.0)

    def cast_units(dst, src, l, u0, u1):
        o = dst[l, u0:u1].rearrange("u p (a b) -> (u p a) b", b=2048)
        i = src[l, u0:u1].rearrange("u p (a b) -> (u p a) b", b=2048)
        dma("pool", o, i)

    def cast_layer(l):
        for u0 in range(0, NU, 4):
            cast_units(wbf, wst, l, u0, min(NU, u0 + 4))

    if NP:
        cast_units(wkvbf, wkv, 0, 0, 4)
    cast_layer(0)
    if NP:
        for l in range(1, L):
            cast_units(wkvbf, wkv, l, 0, 4)
    cast_state = {"next": 1}

    seqs = [("p", i) for i in range(NP)] + ([("s", 0)] if NS else [])
    sched = []
    for kind, si in seqs:
        if kind == "p":
            for l in range(L):
                for u in range(4):
                    sched.append(wkvbf[l, u])
        nt = (SEQ // TT) if kind == "p" else 1
        for t in range(nt):
            for l in range(L):
                for u in range(NU):
                    sched.append(wbf[l, u])
    wstate = {"issued": 0, "used": 0}

    def w_issue():
        i = wstate["issued"]
        if i < len(sched):
            dma("sp", wring[i % NSLOT][:, :], sched[i])
            wstate["issued"] += 1

    def wnext():
        while wstate["issued"] < min(len(sched), wstate["used"] + NSLOT):
            w_issue()
        i = wstate["used"]
        wstate["used"] += 1
        return wring[i % NSLOT]

    def rmsnorm(src, gcol0, T, dst, kc=KC, mark=None):
        m0 = ar_top[0]
        sq = aalloc([128, 2, T], BF16)
        rs = aalloc([128, T])
        ps = bank()
        for j in range(kc):
            s_ = sq[:, j % 2, :]
            act(s_, src[:, j, 0:T], AF.Square)
            mm(ps[:, 0:T], onesb, s_, start=(j == 0), stop=(j == kc - 1))
        act(rs, ps[:, 0:T], AF.Ln, bias=EPS, scale=1.0 / (kc * 128))
        act(rs, rs, AF.Exp, scale=-0.5)
        for j in range(kc):
            stt("dve" if j % 2 == 0 else "pool", dst[:, j, 0:T], src[:, j, 0:T], pc[:, gcol0 + j:gcol0 + j + 1], rs, ALU.mult, ALU.mult)
        ar_top[0] = m0

    def proj(n_units, kc, cols, rhs, T, evac, kouter=False):
        for u in range(n_units):
            w = wnext()
            nmc = cols // 128
            if u == 0 and kouter:
                pss = [bank() for _ in range(nmc)]
                for k in range(kc):
                    for mc in range(nmc):
                        mm(pss[mc][:, 0:T], w[:, k * cols + mc * 128:k * cols + mc * 128 + 128], rhs(k), start=(k == 0), stop=(k == kc - 1))
                for mc in range(nmc):
                    evac(u * nmc + mc, pss[mc][:, 0:T], w)
                continue
            for mc in range(nmc):
                ps = bank()
                for k in range(kc):
                    mm(ps[:, 0:T], w[:, k * cols + mc * 128:k * cols + mc * 128 + 128], rhs(k), start=(k == 0), stop=(k == kc - 1))
                evac(u * nmc + mc, ps[:, 0:T], w)

    PF = AR[:, (AR_BYTES - 16384) // 4:AR_BYTES // 4].rearrange("p (b d) -> p b d", d=D)

    def prefetch_tile(src_rows, T):
        nb = (T + 127) // 128
        rows = min(T, 128)
        dma("sp", PF[0:rows, 0:nb, :], src_rows.rearrange("(b p) d -> p b d", p=rows))
        pf_live[0] = True

    def load_tile_T(src_rows, T, dstT, q="pool", pre=False):
        m0 = ar_top[0]
        nb = (T + 127) // 128
        rows = min(T, 128)
        if pre:
            stg = PF
            pf_live[0] = False
        else:
            stg = aalloc([128, nb, D])
            dma(q, stg[0:rows, :, :], src_rows.rearrange("(b p) d -> p b d", p=rows))
        for j in range(KC):
            ps = bank()
            for b in range(nb):
                tr(ps[:, b * rows:(b + 1) * rows], stg[0:rows, b, j * 128:(j + 1) * 128], identf[0:rows, 0:rows])
            cp("act" if j % 2 == 0 else "dve", dstT[:, j, 0:T], ps[:, 0:T])
        ar_top[0] = m0

    def store_tile_T(srcT, T, dst_rows):
        m0 = ar_top[0]
        nb = (T + 127) // 128
        rows = min(T, 128)
        stg = aalloc([128, nb, D])
        for b in range(nb):
            for half in range(2):
                ps = bank()
                for jj in range(4):
                    j = half * 4 + jj
                    tr(ps[0:rows, jj * 128:(jj + 1) * 128], srcT[:, j, b * rows:(b + 1) * rows], identf)
                cp("act" if half == 0 else "dve", stg[0:rows, b, half * 512:(half + 1) * 512], ps[0:rows, :])
        dma("pool", dst_rows.rearrange("(b p) d -> p b d", p=rows), stg[0:rows, :, :])
        ar_top[0] = m0

    def mem_pass(si):
        m0 = ar_top[0]
        memT = aalloc([128, KC, NMEM])
        load_tile_T(memp[si], NMEM, memT, q=("sp" if si == 0 else "pool"))
        sq = aalloc([128, 2, NMEM], BF16)
        rs = aalloc([128, NMEM])
        ps = bank()
        for j in range(KC):
            act(sq[:, j % 2, :], memT[:, j, :], AF.Square)
            mm(ps[:, 0:NMEM], onesb, sq[:, j % 2, :], start=(j == 0), stop=(j == KC - 1))
        act(rs, ps[:, 0:NMEM], AF.Ln, bias=EPS, scale=1.0 / D)
        act(rs, rs, AF.Exp, scale=-0.5)
        mn = aalloc([128, KC, NMEM], BF16)
        ktb = aalloc([128, KC, NMEM], BF16)
        vb = aalloc([128, 2, D], BF16)
        stg = aalloc([128, 2, D])
        for l in range(L):
            for j in range(KC):
                g = pc[:, l * NPC + PC_NKV + j:l * NPC + PC_NKV + j + 1]
                stt("dve" if j % 2 == 0 else "pool", mn[:, j, :], memT[:, j, :], g, rs, ALU.mult, ALU.mult)
            for which in range(2):
                for half in range(2):
                    w = wnext()
                    if which == 0:
                        for mc in range(4):
                            ps = bank()
                            for k in range(KC):
                                mm(ps[:, 0:NMEM], w[:, k * 512 + mc * 128:k * 512 + mc * 128 + 128], mn[:, k, :], start=(k == 0), stop=(k == KC - 1))
                            cp("act", ktb[:, half * 4 + mc, :], ps[:, 0:NMEM])
                    for blk in range(2):
                        ps = bank()
                        for k in range(KC):
                            mm(ps[:, :], mn[:, k, blk * 128:(blk + 1) * 128], w[:, k * 512:(k + 1) * 512], start=(k == 0), stop=(k == KC - 1))
                        cp("dve", stg[:, blk, half * 512:(half + 1) * 512], ps[:, :])
                        if which == 1:
                            cp("act", vb[:, blk, half * 512:(half + 1) * 512], ps[:, :])
                dst = (o_mk if which == 0 else o_mv)[l, si]
                dma("pool", dst.rearrange("(b p) d -> p b d", p=128), stg[:, :, :])
            dma("pool", kts[l].rearrange("p (a b) -> p a b", b=NMEM), ktb[:, :, :])
            dma("pool", vsc[l].rearrange("p (a b) -> p a b", b=D), vb[:, :, :])
        ar_top[0] = m0

    def sample_kv_pass():
        m0 = ar_top[0]
        stg = aalloc([128, 2, D])
        ktb = aalloc([128, KC, NMEM], BF16)
        for l in range(L):
            dma("pool", stg[:, :, :], cmk[l].rearrange("(b p) d -> p b d", p=128))
            for j in range(KC):
                ps = bank()
                for b in range(2):
                    tr(ps[:, b * 128:(b + 1) * 128], stg[:, b, j * 128:(j + 1) * 128], identf)
                cp("act" if j % 2 == 0 else "dve", ktb[:, j, :], ps[:, 0:NMEM])
            dma("pool", kts[l].rearrange("p (a b) -> p a b", b=NMEM), ktb[:, :, :])
            dma("pool", vsc[l].rearrange("p (b d) -> p b d", d=D), cmv[l].rearrange("(b p) d -> p b d", p=128))
        ar_top[0] = m0

    def layer(l, T, last_tile, oslot, ffn_hook=None):
        C = min(128, T)
        NCH = T // C
        pcl = l * NPC
        prl = l * NPR
        m_layer = ar_top[0]
        rmsnorm(xT, pcl + PC_NMIX, T, xn)
        qT = aalloc([128, 4, T], BF16)
        kT = aalloc([128, 4, T], BF16)
        vT = aalloc([128, 4, T], BF16)
        zaT = aalloc([128, 4, T], BF16)
        zbT = aalloc([128, 4, T], BF16)
        qkb = aalloc([128, 4, T])
        vbT = aalloc([128, 4, T], BF16)
        glrT = aalloc([128, T], BF16)
        sp_ = aalloc([128, NCH, 256])
        bet = aalloc([128, NCH, 4])
        gg = aalloc([128, NCH, 4])
        oaT = aalloc([128, 4, T], BF16)
        obT = aalloc([128, 4, T], BF16)
        mrg = aalloc([128, KC, T], BF16)
        sgb = aalloc([128, KC, T], BF16)
        tg = aalloc([128, 2, T], BF16)
        m_tmp = ar_top[0]
        raw = aalloc([128, 4, 3 + T])
        cv = aalloc([128, 4, T])
        sq = aalloc([128, 2, T], BF16)
        rs = aalloc([128, 2, T])
        tmp4 = aalloc([128, 8])

        def rhs_xn(k):
            return xn[:, k, 0:T]

        sl8 = aalloc([128, 8, T])

        def stageB(j):
            s_ = sl8[:, j, :]
            q_ = sq[:, j % 2, :]
            tt("pool", q_, s_, s_, ALU.mult)
            ps2 = bank()
            mm(ps2[:, 0:T], onesb, q_)
            r_ = rs[:, j % 2, :]
            act(r_, ps2[:, 0:T], AF.Ln, bias=EPS)
            act(r_, r_, AF.Exp, scale=-0.5)
            if j < 4:
                stt("dve", qT[:, j, :], s_, 128.0 ** -0.5, r_, ALU.mult, ALU.mult)
            else:
                tt("dve", kT[:, j - 4, :], s_, r_, ALU.mult)

        def evac_qkv(j, ps, w):
            r = raw[:, j % 4, :]
            cw = pcl + PC_CONVA + j * 4
            c_ = cv[:, j % 4, :]
            cp("pool", r[:, 0:3], car_a[:, l, j, :])
            cp("act", r[:, 3:3 + T], ps)
            act(c_, ps, AF.Identity, scale=pc[:, cw + 3:cw + 4])
            cp("pool", car_a[:, l, j, :], r[:, T:T + 3])
            for i in range(3):
                stt("dve", c_, r[:, i:i + T], pc[:, cw + i:cw + i + 1], c_, ALU.mult, ALU.add)
            def tail(j=j, c_=c_):
                if j >= 8:
                    act(vT[:, j - 8, :], c_, AF.Silu)
                else:
                    act(sl8[:, j, :], c_, AF.Silu)
            pend_q.append(tail)
            while len(pend_q) > 2:
                pend_q.pop(0)()

        if last_tile:
            stg3 = aalloc([128, 1536])
        proj_state = {}

        def evac_qkv_w(j, ps, w):
            evac_qkv(j, ps, w)
            if last_tile and j % 4 == 3:
                u = j // 4
                ps3 = bank()
                for k in range(KC):
                    mm(ps3[0:3, :], xn[:, k, T - 3:T], w[:, k * 512:(k + 1) * 512], start=(k == 0), stop=(k == KC - 1))
                cp("act", stg3[0:3, u * 512:(u + 1) * 512], ps3[0:3, :])
                if u == 2:
                    dma("pool", o_gdc[l, oslot], stg3[0:3, :])

        pend_q = []
        proj(3, KC, 512, rhs_xn, T, evac_qkv_w, kouter=True)
        while pend_q:
            pend_q.pop(0)()

        def evac_silu(dst):
            def f(j, ps, w):
                act(dst[:, j, :], ps, AF.Silu)
            return f

        def evac_qkb(j, ps, w):
            cp("act" if j % 2 == 0 else "dve", qkb[:, j, :], ps)
        proj(1, KC, 512, rhs_xn, T, evac_qkb)

        def evac_vb(j, ps, w):
            cp("act" if j % 2 == 0 else "dve", vbT[:, j, :], ps)
        proj(1, KC, 512, rhs_xn, T, evac_vb)
        proj(1, KC, 512, rhs_xn, T, evac_silu(zaT))
        proj(1, KC, 512, rhs_xn, T, evac_silu(zbT))
        for jj in range(8):
            stageB(jj)
        ps = bank()
        for k in range(KC):
            mm(ps[0:16, 0:T], wsmb[:, l, k * 24:k * 24 + 16], xn[:, k, 0:T], start=(k == 0), stop=(k == KC - 1))
        cp("act", glrT[0:16, :], ps[0:16, 0:T])
        for c in range(NCH):
            ps = bank()
            for k in range(KC):
                mm(ps[0:C, 0:8], xn[:, k, c * C:(c + 1) * C], wsmb[:, l, k * 24 + 16:k * 24 + 24], start=(k == 0), stop=(k == KC - 1))
            act(tmp4[0:C, 0:4], ps[0:C, 0:4], AF.Exp, scale=-1.0)
            ts("dve", tmp4[0:C, 0:4], tmp4[0:C, 0:4], 1.0, None, ALU.add)
            recip(bet[0:C, c, :], tmp4[0:C, 0:4])
            tt("dve", tmp4[0:C, 4:8], ps[0:C, 4:8], pr[0:C, prl + 4:prl + 8], ALU.add)
            act(tmp4[0:C, 4:8], tmp4[0:C, 4:8], AF.Exp)
            act(tmp4[0:C, 4:8], tmp4[0:C, 4:8], AF.Ln, bias=1.0)
            tt("dve", gg[0:C, c, :], tmp4[0:C, 4:8], nega[0:C, l * 4:l * 4 + 4], ALU.mult)
            ps2 = bank()
            mm(ps2[0:C, 0:256], glrT[0:16, c * C:(c + 1) * C], w2b[:, l * 256:(l + 1) * 256])
            tt("dve", sp_[0:C, c, :], ps2[0:C, 0:256], pr[0:C, prl + 8:prl + 264], ALU.add)
            act(sp_[0:C, c, :], sp_[0:C, c, :], AF.Exp, scale=-1.0)
            act(sp_[0:C, c, :], sp_[0:C, c, :], AF.Ln, bias=1.0)

        ar_top[0] = m_tmp
        ckpt('front')

        m_c = ar_top[0]
        Gcol = aalloc([128, 2, 4])
        gb = aalloc([128, 4, 128])
        EG = aalloc([128, 2, 4, C])
        DT = aalloc([128, 4, C])
        Dm = aalloc([128, 4, C])
        Psets = [[aalloc([128, 4, C]) for _ in range(5)] for _ in range(2)]
        TTb = aalloc([128, 2, 4, C], BF16)
        AqT = aalloc([128, 2, 4, C], BF16)
        QgT = aalloc([128, 2, 4, C], BF16)
        Vb = aalloc([128, 2, 4, 128])
        Kg = aalloc([128, 2, 4, 128], BF16)
        Rt = aalloc([128, 4, 128])
        Rb = aalloc([128, 4, 128], BF16)
        vnb = aalloc([128, 4, 128], BF16)
        sm = aalloc([128, 2, 16])
        nsteps = {128: 6, 64: 5}[C]
        st_ = {"pdone": set(), "c_done": 0, "early": 0, "z": 2, "fill": 8, "oa": 0, "ob": 0}
        sqa = aalloc([128, 4 * C], BF16)
        sqb = aalloc([128, 4 * C], BF16)
        rsb = aalloc([128, 4, C])

        rsa = aalloc([128, 4, C])

        def norm_stream(key, oT_, zT_, gc, sq_, rs_, zneed):
            for c in range(NCH):
                while st_[key] <= c or st_["z"] < zneed:
                    yield "blocked"
                ck = slice(c * C, (c + 1) * C)
                o3 = oT_[:, :, ck]
                act(sq_.rearrange("p (h c) -> p h c", c=C), o3, AF.Square)
                yield None
                psq = bank()
                mm(psq[:, 0:4 * C], onesb, sq_)
                act(rs_, psq[:, 0:4 * C].rearrange("p (h c) -> p h c", c=C), AF.Ln, bias=EPS, scale=1.0 / 128)
                act(rs_, rs_, AF.Exp, scale=-0.5)
                yield None
                stt("dve", o3, o3, pc[:, pcl + gc:pcl + gc + 1], rs_, ALU.mult, ALU.mult)
                tt("pool", o3, o3, zT_[:, :, ck], ALU.mult)
                yield None

        def gdn_prep(c):
            B = c % 2
            Pa, Pta, Pb, Ptb, X = Psets[B]
            t0 = c * C
            ck = slice(t0, t0 + C)
            ps = bank()
            mm(ps[0:C, 0:4], triU[0:C, 0:C], gg[0:C, c, :])
            cp("act", Gcol[0:C, B, :], ps[0:C, 0:4])
            cp("pool", gb[0:C, :, :], gg[0:C, c, :].unsqueeze(2).to_broadcast([C, 4, 128]))
            yield
            psG = bank()
            for h in range(4):
                mm(psG[:, h * C:(h + 1) * C], gb[0:C, h, :], triU[0:C, 0:C])
            psG3 = psG[:, 0:4 * C].rearrange("p (h c) -> p h c", c=C)
            act(EG[:, B, :, :], psG3, AF.Exp)
            for h in range(4):
                stt("dve", DT[0:C, h, :], psG3[0:C, h, :], Gcol[0:C, B, h:h + 1], NEGT[0:C, 0:C], ALU.subtract, ALU.add)
                stt("dve", Dm[0:C, h, :], psG3[0:C, h, :], Gcol[0:C, B, h:h + 1], POSL[0:C, 0:C], ALU.subtract, ALU.add)
            tt("dve", sm[0:C, B, 4:8], psG3[0:C, :, C - 1], Gcol[0:C, B, :], ALU.subtract)
            yield
            act(DT[0:C, :, :], DT[0:C, :, :], AF.Exp)
            act(Dm[0:C, :, :], Dm[0:C, :, :], AF.Exp, scale=-1.0)
            ts("dve", sm[0:C, B, 0:4], bet[0:C, c, :], -1.0)
            act(sm[0:C, B, 4:8], sm[0:C, B, 4:8], AF.Exp)
            act(sm[0:C, B, 8:12], Gcol[0:C, B, :], AF.Exp)
            tt("dve", sm[0:C, B, 8:12], sm[0:C, B, 8:12], bet[0:C, c, :], ALU.mult)
            yield
            psK = bank()
            for h in range(4):
                mm(psK[0:C, h * C:(h + 1) * C], kT[:, h, ck], kT[:, h, ck])
            for h in range(4):
                stt("dve", Pa[0:C, h, :], psK[0:C, h * C:(h + 1) * C], sm[0:C, B, h:h + 1], Dm[0:C, h, :], ALU.mult, ALU.mult)
            yield
            psT = bank()
            for h in range(4):
                tr(psT[0:C, h * C:(h + 1) * C], Pa[0:C, h, :], identf[0:C, 0:C])
            psT3 = psT[0:C, 0:4 * C].rearrange("p (h c) -> p h c", c=C)
            cp("act", Pta[0:C, :, :], psT3)
            tt("dve", X[0:C, :, :], psT3, identf[0:C, 0:C].unsqueeze(1).to_broadcast([C, 4, C]), ALU.add)
            yield
            psA = bank()
            for h in range(4):
                mm(psA[0:C, h * C:(h + 1) * C], kT[:, h, ck], qT[:, h, ck])
            tt("dve", AqT[0:C, B, :, :], psA[0:C, 0:4 * C].rearrange("p (h c) -> p h c", c=C), DT[0:C, :, :], ALU.mult)
            tt("pool", QgT[:, B, :, :], qT[:, :, ck], EG[:, B, :, :], ALU.mult)
            yield
            psV = bank()
            psVb = psV[:, :].bitcast(BF16)
            for h in range(4):
                tr(psVb[0:C, h * 128:(h + 1) * 128], vT[:, h, ck], identb)
            tt("dve", Vb[0:C, B, :, :], psVb[0:C, 0:512].rearrange("p (h d) -> p h d", d=128),
               bet[0:C, c, :].unsqueeze(2).to_broadcast([C, 4, 128]), ALU.mult)
            yield
            psKt = bank()
            psKb = psKt[:, :].bitcast(BF16)
            for h in range(4):
                tr(psKb[0:C, h * 128:(h + 1) * 128], kT[:, h, ck], identb)
            tt("dve", Kg[0:C, B, :, :], psKb[0:C, 0:512].rearrange("p (h d) -> p h d", d=128),
               sm[0:C, B, 4:8].unsqueeze(2).to_broadcast([C, 4, 128]), ALU.mult)
            st_["early"] = c + 1
            yield
            Pc, Ptc, Pn, Ptn = Pa, Pta, Pb, Ptb
            for st in range(nsteps):
                lastst = st == nsteps - 1
                ps1 = bank()
                for h in range(4):
                    mm(ps1[0:C, h * C:(h + 1) * C], Ptc[0:C, h, :], Pc[0:C, h, :])
                cp("act", Pn[0:C, :, :], ps1[0:C, 0:4 * C].rearrange("p (h c) -> p h c", c=C))
                if not lastst:
                    ps2 = bank()
                    for h in range(4):
                        mm(ps2[0:C, h * C:(h + 1) * C], Pc[0:C, h, :], Ptc[0:C, h, :])
                    cp("dve", Ptn[0:C, :, :], ps2[0:C, 0:4 * C].rearrange("p (h c) -> p h c", c=C))
                yield
                ps3 = bank()
                for h in range(4):
                    mm(ps3[0:C, h * C:(h + 1) * C], Pn[0:C, h, :], X[0:C, h, :])
                tt("dve", X[0:C, :, :], X[0:C, :, :], ps3[0:C, 0:4 * C].rearrange("p (h c) -> p h c", c=C), ALU.add)
                Pc, Ptc, Pn, Ptn = Pn, Ptn, Pc, Ptc
                yield
            cp("act", TTb[0:C, B, :, :], X[0:C, :, :])
            yield

        def gdn_chain(c):
            B = c % 2
            t0 = c * C
            ck = slice(t0, t0 + C)
            psP = bank()
            for h in range(4):
                mm(psP[0:C, h * 128:(h + 1) * 128], kT[:, h, ck], Sgb[:, l, h, :])
            tt("dve", Rt[0:C, :, :], psP[0:C, :].rearrange("p (h d) -> p h d", d=128),
               sm[0:C, B, 8:12].unsqueeze(2).to_broadcast([C, 4, 128]), ALU.mult)
            tt("pool", Rb[0:C, :, :], Vb[0:C, B, :, :], Rt[0:C, :, :], ALU.subtract)
            yield
            psN = bank()
            for h in range(4):
                mm(psN[0:C, h * 128:(h + 1) * 128], TTb[0:C, B, h, :], Rb[0:C, h, :])
            cp("act", vnb[0:C, :, :], psN[0:C, :].rearrange("p (h d) -> p h d", d=128))
            yield
            psO = bank()
            for h in range(4):
                mm(psO[:, h * C:(h + 1) * C], Sgb[:, l, h, :], QgT[:, B, h, :], start=True, stop=False)
                mm(psO[:, h * C:(h + 1) * C], vnb[0:C, h, :], AqT[0:C, B, h, :], start=False, stop=True)
            cp("act", oaT[:, :, ck], psO[:, 0:4 * C].rearrange("p (h c) -> p h c", c=C))
            st_["oa"] = c + 1
            yield
            psS = bank()
            for h in range(4):
                mm(psS[:, h * 128:(h + 1) * 128], Kg[0:C, B, h, :], vnb[0:C, h, :])
            tt("pool", Sg[:, l, :, :], Sg[:, l, :, :], EG[:, B, :, C - 1:C].to_broadcast([128, 4, 128]), ALU.mult)
            tt("dve", Sg[:, l, :, :], Sg[:, l, :, :], psS[:, :].rearrange("p (h d) -> p h d", d=128), ALU.add)
            cp("act", Sgb[:, l, :, :], Sg[:, l, :, :])
            yield

        eG = aalloc([128, 2, C])
        enG = aalloc([128, 2, C])
        ekl = aalloc([128, 2, C])
        glc = aalloc([128, 4])
        qgz = aalloc([128, 4, C], BF16)
        memset("pool", qgz[:, :, :], 0.0)
        kng = aalloc([128, 2, C], BF16)
        kg = aalloc([128, 2, C], BF16)
        KgT = aalloc([128, 2, 128], BF16)
        Vt = aalloc([128, 4, 128], BF16)
        ATb = aalloc([128, 4, C], BF16)

        def gla_chunk(c):
            t0 = c * C
            ck = slice(t0, t0 + C)
            psG = bank()
            for fc in range(2):
                mm(psG[:, fc * C:(fc + 1) * C], sp_[0:C, c, fc * 128:(fc + 1) * 128], triU[0:C, 0:C])
            psG3 = psG[:, 0:2 * C].rearrange("p (f c) -> p f c", c=C)
            act(eG[:, :, :], psG3, AF.Exp, scale=-1.0 / 16)
            act(enG[:, :, :], psG3, AF.Exp, scale=1.0 / 16)
            ts("dve", glc[:, 0:2], psG3[:, :, C - 1], -1.0 / 16)
            yield
            for fc in range(2):
                act(ekl[:, fc, :], psG3[:, fc, :], AF.Exp, scale=1.0 / 16, bias=glc[:, fc:fc + 1])
            act(glc[:, 2:4], glc[:, 0:2], AF.Exp)
            for h in range(4):
                fc, off = h // 2, 64 * (h % 2)
                stt("dve", qgz[off:off + 64, h, :], qkb[off:off + 64, fc, ck], 0.125, eG[off:off + 64, fc, :], ALU.mult, ALU.mult)
            tt("pool", kng[:, :, :], qkb[:, 2:4, ck], enG[:, :, :], ALU.mult)
            tt("pool", kg[:, :, :], qkb[:, 2:4, ck], ekl[:, :, :], ALU.mult)
            yield
            psK = bank()
            psKb = psK[:, :].bitcast(BF16)
            for fc in range(2):
                tr(psKb[0:C, fc * 128:(fc + 1) * 128], kg[:, fc, :], identb)
            cp("act", KgT[0:C, :, :], psKb[0:C, 0:256].rearrange("p (f d) -> p f d", d=128))
            yield
            psV = bank()
            psVb = psV[:, :].bitcast(BF16)
            for h in range(4):
                tr(psVb[0:C, h * 128:(h + 1) * 128], vbT[:, h, ck], identb)
            cp("dve", Vt[0:C, :, :], psVb[0:C, 0:512].rearrange("p (h d) -> p h d", d=128))
            yield
            psA = bank()
            for h in range(4):
                fc = h // 2
                mm(psA[0:C, h * C:(h + 1) * C], kng[:, fc, :], qgz[:, h, :])
            tt("dve", ATb[0:C, :, :], psA[0:C, 0:4 * C].rearrange("p (h c) -> p h c", c=C),
               triU[0:C, 0:C].unsqueeze(1).to_broadcast([C, 4, C]), ALU.mult)
            yield
            psO = bank()
            for h in range(4):
                fc = h // 2
                mm(psO[:, h * C:(h + 1) * C], Slb[:, l, fc, :], qgz[:, h, :], start=True, stop=False)
                mm(psO[:, h * C:(h + 1) * C], Vt[0:C, h, :], ATb[0:C, h, :], start=False, stop=True)
            cp("act", obT[:, :, ck], psO[:, 0:4 * C].rearrange("p (h c) -> p h c", c=C))
            st_["ob"] = c + 1
            yield
            psS = bank()
            for fc in range(2):
                mm(psS[:, fc * 256:(fc + 1) * 256], KgT[0:C, fc, :], Vt[0:C, 2 * fc:2 * fc + 2, :])
            for h in range(4):
                fc, off = h // 2, 64 * (h % 2)
                stt("dve", Sl[off:off + 64, l, fc, :], Sl[off:off + 64, l, fc, :], glc[off:off + 64, 2 + fc:3 + fc],
                    psS[off:off + 64, fc * 256 + (h % 2) * 128:fc * 256 + (h % 2) * 128 + 128], ALU.mult, ALU.add)
            cp("act", Slb[:, l, :, :], Sl[:, l, :, :])
            yield

        def prep_par(par):
            for c in range(par, NCH, 2):
                while c >= st_["c_done"] + 2 or st_["early"] < c:
                    yield "blocked"
                for _ in gdn_prep(c):
                    yield None
                st_["pdone"].add(c)

        def chain_all():
            for c in range(NCH):
                while c not in st_["pdone"]:
                    yield "blocked"
                for _ in gdn_chain(c):
                    yield None
                st_["c_done"] = c + 1

        def gla_all():
            for c in range(NCH):
                for _ in gla_chunk(c):
                    yield None

        def gates_all():
            for gi_, dst in enumerate((mrg, sgb)):
                for uu in range(2):
                    w = wnext()
                    for mc in range(4):
                        j = uu * 4 + mc
                        ps = bank()
                        for k in range(KC):
                            mm(ps[:, 0:T], w[:, k * 512 + mc * 128:k * 512 + mc * 128 + 128], xn[:, k, 0:T], start=(k == 0), stop=(k == KC - 1))
                        cp("act", dst[:, j, :], ps[:, 0:T])
                        st_["fill"] += 1
                        yield None

        gens = [prep_par(0), prep_par(1), gla_all(), chain_all(), gates_all(),
                norm_stream("oa", oaT, zaT, PC_ONA, sqa, rsa[:, :, :], 1),
                norm_stream("ob", obT, zbT, PC_ONB, sqb, rsb[:, :, :], 2)]
        alive = [True] * 7
        rnd = 0
        while any(alive):
            rnd += 1
            for gi, g in enumerate(gens):
                if not alive[gi]:
                    continue
                if gi == 4 and st_["z"] >= 2 and (any(alive[0:4]) or any(alive[5:7])):
                    if any(alive[0:3]):
                        if rnd % 3 != 0 or st_["fill"] >= 16:
                            continue
                try:
                    next(g)
                    if gi == 4 and not any(alive[0:3]) and (alive[3] or alive[5] or alive[6]):
                        next(g)
                except StopIteration:
                    alive[gi] = False
        for dst in (mrg, sgb):
            for j in range(KC):
                act(dst[:, j, :], dst[:, j, :], AF.Tanh, scale=0.5)
                ts("dve" if j % 2 == 0 else "pool", dst[:, j, :], dst[:, j, :], 0.5, 0.5, ALU.mult, ALU.add)
        ar_top[0] = m_c
        ckpt('gla')


        def evac_ya(j, ps, w):
            tt("dve", mrg[:, j, :], mrg[:, j, :], ps, ALU.mult)
        proj(2, 4, 512, lambda k: oaT[:, k, :], T, evac_ya)

        def evac_yb(j, ps, w):
            tt("dve", sgb[:, j, :], sgb[:, j, :], ps, ALU.mult)
            tt("pool", mrg[:, j, :], mrg[:, j, :], sgb[:, j, :], ALU.add)
        proj(2, 4, 512, lambda k: obT[:, k, :], T, evac_yb)

        def evac_res(j, ps, w):
            tt("dve", xT[:, j, 0:T], xT[:, j, 0:T], ps, ALU.add)
        proj(2, KC, 512, lambda k: mrg[:, k, :], T, evac_res)
        ar_top[0] = m_layer
        ckpt('mixer')

        KTt = aalloc([128, KC, NMEM], BF16)
        Vtk = aalloc([128, 2, D], BF16)
        dma("pool", KTt[:, :, :], kts[l].rearrange("p (a b) -> p a b", b=NMEM))
        dma("pool", Vtk[:, :, :], vsc[l].rearrange("p (a b) -> p a b", b=D))
        rmsnorm(xT, pcl + PC_NMEM, T, xn)
        q2 = aalloc([128, KC, T], BF16)

        def evac_q2(j, ps, w):
            if j % 2 == 0:
                act(q2[:, j, :], ps, AF.Copy, scale=1.0 / 16)
            else:
                ts("dve", q2[:, j, :], ps, 1.0 / 16)
        proj(2, KC, 512, rhs_xn, T, evac_q2, kouter=True)
        oT2 = aalloc([128, KC, T], BF16)
        aT = aalloc([128, 2, 2, T], BF16)
        Ee = aalloc([128, 4, NMEM])
        ab = aalloc([128, 4, NMEM], BF16)
        st4 = aalloc([128, 16])
        NB = T // C
        items = [(h, b) for h in range(4) for b in range(NB)]

        def at_stage1(i):
            h, b = items[i]
            i4 = i % 4
            bk = slice(b * C, (b + 1) * C)
            ps = bank()
            for dc in range(2):
                mm(ps[0:C, 0:NMEM], q2[:, 2 * h + dc, bk], KTt[:, 2 * h + dc, :], start=(dc == 0), stop=(dc == 1))
            rmax(st4[0:C, i4:i4 + 1], ps[0:C, 0:NMEM])
            ts("dve", st4[0:C, 4 + i4:5 + i4], st4[0:C, i4:i4 + 1], -1.0)
            act(Ee[0:C, i4, :], ps[0:C, 0:NMEM], AF.Exp, bias=st4[0:C, 4 + i4:5 + i4], accum=st4[0:C, 8 + i4:9 + i4])
            recip(st4[0:C, 12 + i4:13 + i4], st4[0:C, 8 + i4:9 + i4])
            ts("dve", ab[0:C, i4, :], Ee[0:C, i4, :], st4[0:C, 12 + i4:13 + i4])

        def at_stage2(i):
            h, b = items[i]
            i4 = i % 4
            bk = slice(b * C, (b + 1) * C)
            psT = bank()
            psTb = psT[:, :].bitcast(BF16)
            for mc in range(2):
                tr(psTb[:, mc * C:(mc + 1) * C], ab[0:C, i4, mc * 128:(mc + 1) * 128], identb[0:C, 0:C])
            cp("act", aT[:, h % 2, :, bk], psTb[:, 0:2 * C].rearrange("p (m c) -> p m c", c=C))
            if b == NB - 1:
                for dc in range(2):
                    ps = bank()
                    for mc in range(2):
                        mm(ps[:, 0:T], Vtk[:, mc, (2 * h + dc) * 128:(2 * h + dc + 1) * 128], aT[:, h % 2, mc, :], start=(mc == 0), stop=(mc == 1))
                    cp("dve", oT2[:, 2 * h + dc, :], ps[:, 0:T])

        nit = len(items)
        for i in range(min(3, nit)):
            at_stage1(i)
        for i in range(nit):
            if i + 3 < nit:
                at_stage1(i + 3)
            at_stage2(i)
        proj(2, KC, 512, lambda k: oT2[:, k, :], T, evac_res)
        ar_top[0] = m_layer
        ckpt('attn')

        if ffn_hook is not None:
            ffn_hook()
        rmsnorm(xT, pcl + PC_NFFN, T, xn)
        hT = aalloc([128, 22, T], BF16)
        raw2 = aalloc([128, 8, 2 + T])
        cu = aalloc([128, 8, T])
        if last_tile:
            stg2 = aalloc([128, 2, 512])

        def evac_up(jj, ps, w):
            u, i = jj // 4, jj % 4
            ch = (2 * u + i) if i < 2 else (22 + 2 * u + (i - 2))
            i8 = (u % 2) * 4 + i
            r = raw2[:, i8, :]
            cp("pool", r[:, 0:2], car_f[:, l, ch, :])
            cp("act", r[:, 2:2 + T], ps)
            cp("pool", car_f[:, l, ch, :], r[:, T:T + 2])
            cw = pcl + PC_CFW + ch * 3
            cbb = pcl + PC_CFB + ch
            c_ = cu[:, i8, :]
            act(c_, ps, AF.Identity, bias=pc[:, cbb:cbb + 1], scale=pc[:, cw + 2:cw + 3])
            stt("dve", c_, r[:, 1:1 + T], pc[:, cw + 1:cw + 2], c_, ALU.mult, ALU.add)
            stt("dve", c_, r[:, 0:T], pc[:, cw:cw + 1], c_, ALU.mult, ALU.add)
            def tail(i=i, u=u, i8=i8, c_=c_):
                if i < 2:
                    act(c_, c_, AF.Silu)
                else:
                    tt("pool", hT[:, 2 * u + (i - 2), :], cu[:, i8 - 2, :], c_, ALU.mult)
            pend_f.append(tail)
            while len(pend_f) > 4:
                pend_f.pop(0)()
            if last_tile and i == 3:
                ps3 = bank()
                for k in range(KC):
                    mm(ps3[0:2, :], xn[:, k, T - 2:T], w[:, k * 512:(k + 1) * 512], start=(k == 0), stop=(k == KC - 1))
                sg_ = stg2[0:2, u % 2, :]
                cp("act", sg_, ps3[0:2, :])
                dma("pool", o_ffc[l, oslot, :, 256 * u:256 * u + 256], sg_[:, 0:256])
                dma("pool", o_ffc[l, oslot, :, DFF + 256 * u:DFF + 256 * u + 256], sg_[:, 256:512])

        pend_f = []
        proj(11, KC, 512, rhs_xn, T, evac_up, kouter=True)
        while pend_f:
            pend_f.pop(0)()
        for half in range(2):
            pss = [bank() for _ in range(4)]
            for kg_ in range(3):
                w = wnext()
                nk = 8 if kg_ < 2 else 6
                for mc in range(4):
                    for kk in range(nk):
                        k = kg_ * 8 + kk
                        mm(pss[mc][:, 0:T], w[:, kk * 512 + mc * 128:kk * 512 + mc * 128 + 128], hT[:, k, :], start=(k == 0), stop=(k == 21))
            for mc in range(4):
                evac_res(half * 4 + mc, pss[mc][:, 0:T], None)
        ar_top[0] = m_layer

    tile_list = []
    for kind_, si_ in seqs:
        for t_ in range((SEQ // TT) if kind_ == "p" else 1):
            tile_list.append((kind_, si_, t_))

    def main_loop():
        for kind, si in seqs:
            oslot = si if kind == "p" else NP
            T = TT if kind == "p" else 64
            nt = (SEQ // TT) if kind == "p" else 1
            if kind == "p":
                memset("pool", Sg[:, :, :, :], 0.0)
                memset("pool", Sgb[:, :, :, :], 0.0)
                memset("pool", Sl[:, :, :, :], 0.0)
                memset("pool", Slb[:, :, :, :], 0.0)
                memset("pool", car_a[:, :, :, :], 0.0)
                memset("pool", car_f[:, :, :, :], 0.0)
                mem_pass(si)
                ckpt('mem')
            else:
                for l in range(L):
                    dma("pool", Sg[:, l, :, :], sgdn[l].rearrange("h k v -> k h v"))
                    for fc in range(2):
                        dma("pool", Sl[:, l, fc, :], sgla[l, 2 * fc:2 * fc + 2].rearrange("h d v -> (h d) v"))
                    dma("pool", car_a[:, l, :, :], sgdc[l].rearrange("p (a b) -> p a b", b=3))
                    dma("pool", car_f[:, l, :, :], sffc[l].rearrange("p (a b) -> p a b", b=2))
                cp("dve", Sgb[:, :, :, :], Sg[:, :, :, :])
                cp("dve", Slb[:, :, :, :], Sl[:, :, :, :])
                sample_kv_pass()
            for t in range(nt):
                src = xp[si, t * T:(t + 1) * T, :] if kind == "p" else xs[:, :]
                gi_ = tile_list.index((kind, si, t))
                load_tile_T(src, T, xT, q="sp", pre=(gi_ > 0))
                ckpt('load')
                hook = None
                if gi_ + 1 < len(tile_list):
                    k2, s2, t2 = tile_list[gi_ + 1]
                    T2 = TT if k2 == "p" else 64
                    src2 = xp[s2, t2 * T2:(t2 + 1) * T2, :] if k2 == "p" else xs[:, :]
                    hook = (lambda src2=src2, T2=T2: prefetch_tile(src2, T2))
                for l in range(L):
                    if cast_state["next"] == l + 1 and l + 1 < L:
                        cast_layer(l + 1)
                        cast_state["next"] = l + 2
                    layer(l, T, t == nt - 1, oslot, ffn_hook=(hook if l == L - 1 else None))
                m0 = ar_top[0]
                xo = aalloc([128, KC, T])
                rmsnorm(xT, PC_NFIN, T, xo)
                dst = yp[si, t * T:(t + 1) * T, :] if kind == "p" else ys[:, :]
                store_tile_T(xo, T, dst)
                ar_top[0] = m0
            for l in range(L):
                dma("pool", o_gdn[l, oslot].rearrange("h k v -> k h v"), Sg[:, l, :, :])
                for fc in range(2):
                    dma("pool", o_gla[l, oslot, 2 * fc:2 * fc + 2].rearrange("h d v -> (h d) v"), Sl[:, l, fc, :])

    try:
        ckpt('prologue')
        main_loop()
    except StopBuild:
        pass
    P.emit(nc, ES)
    ES.close()
    return nc, P


def _consts():
    c = np.zeros((128, 640), np.float32)
    i = np.arange(128)
    c[:, 0:128] = np.eye(128)
    c[:, 128:256] = (i[:, None] <= i[None, :])
    c[:, 256:384] = np.where(i[None, :] >= i[:, None], 0.0, -BIG)
    c[:, 384:512] = np.where(i[:, None] > i[None, :], 0.0, BIG)
    c[:, 512:640] = 1.0
    return c


def _prep_weights(inp, L):
    f = np.float32
    w_in = np.asarray(inp["w_in"], f)
    wst = np.zeros((L, NU, 128, UNIT), f)

    def unit_k8(W):
        return W.reshape(8, 128, 512).transpose(1, 0, 2).reshape(128, 4096)

    def unit_k4(W):
        o = np.zeros((128, 4096), f)
        o[:, :2048] = W.reshape(4, 128, 512).transpose(1, 0, 2).reshape(128, 2048)
        return o

    for l in range(L):
        wi = w_in[l]
        cols = [(0, 512), (512, 1024), (1024, 1536), (2056, 2568), (2568, 3080), (1544, 2056), (3096, 3608),
                (3608, 4120), (4120, 4632), (4632, 5144), (5144, 5656)]
        u = 0
        for a, b in cols:
            wst[l, u] = unit_k8(wi[:, a:b]); u += 1
        woa = np.asarray(inp["w_out_a"][l], f)
        wst[l, u] = unit_k4(woa[:, 0:512]); u += 1
        wst[l, u] = unit_k4(woa[:, 512:1024]); u += 1
        wob = np.asarray(inp["w_out_b"][l], f)
        wst[l, u] = unit_k4(wob[:, 0:512]); u += 1
        wst[l, u] = unit_k4(wob[:, 512:1024]); u += 1
        for name in ("w_o", "w_mq", "w_mo"):
            W = np.asarray(inp[name][l], f)
            wst[l, u] = unit_k8(W[:, 0:512]); u += 1
            wst[l, u] = unit_k8(W[:, 512:1024]); u += 1
        wu = np.asarray(inp["w_up"][l], f)
        for uu in range(11):
            W = np.concatenate([wu[:, 256 * uu:256 * uu + 256], wu[:, DFF + 256 * uu:DFF + 256 * uu + 256]], axis=1)
            wst[l, u] = unit_k8(W); u += 1
        wd = np.asarray(inp["w_down"][l], f)
        for half in range(2):
            for kg in range(3):
                nk = 8 if kg < 2 else 6
                o = np.zeros((128, 8, 512), f)
                o[:, :nk, :] = wd[kg * 1024:kg * 1024 + nk * 128, half * 512:(half + 1) * 512].reshape(nk, 128, 512).transpose(1, 0, 2)
                wst[l, u] = o.reshape(128, 4096); u += 1
        assert u == NU
    wkv = np.zeros((L, 4, 128, UNIT), f)
    for l in range(L):
        for wi_, name in enumerate(("w_mk", "w_mv")):
            W = np.asarray(inp[name][l], f)
            wkv[l, 2 * wi_] = unit_k8(W[:, 0:512])
            wkv[l, 2 * wi_ + 1] = unit_k8(W[:, 512:1024])
    wsm = np.zeros((L, 128, 8, 24), f)
    for l in range(L):
        wi = w_in[l]
        small = np.concatenate([wi[:, 3080:3096], wi[:, 1536:1544]], axis=1)
        wsm[l] = small.reshape(8, 128, 24).transpose(1, 0, 2)
    wsm = wsm.reshape(L, 128, 192)
    w2 = np.asarray(inp["w_gate_b2"], f)[:L].transpose(1, 0, 2).reshape(16, L * 256)
    pcol = np.zeros((128, L, NPC), f)

    def colmaj(v):
        return np.asarray(v, f).reshape(-1, 128).T

    for l in range(L):
        pcol[:, l, PC_NMIX:PC_NMIX + 8] = colmaj(inp["norm_mix"][l])
        pcol[:, l, PC_NMEM:PC_NMEM + 8] = colmaj(inp["norm_mem"][l])
        pcol[:, l, PC_NFFN:PC_NFFN + 8] = colmaj(inp["norm_ffn"][l])
        pcol[:, l, PC_NKV:PC_NKV + 8] = colmaj(inp["norm_memkv"][l])
        ca = np.asarray(inp["conv_a_w"][l], f)
        pcol[:, l, PC_CONVA:PC_CONVA + 48] = ca.reshape(4, 12, 128).transpose(2, 1, 0).reshape(128, 48)
        pcol[:, l, PC_ONA] = np.asarray(inp["onorm_a"][l], f)
        pcol[:, l, PC_ONB] = np.asarray(inp["onorm_b"][l], f)
        cf = np.asarray(inp["conv_f_w"][l], f)
        pcol[:, l, PC_CFW:PC_CFW + 132] = cf.reshape(3, 44, 128).transpose(2, 1, 0).reshape(128, 132)
        pcol[:, l, PC_CFB:PC_CFB + 44] = colmaj(inp["conv_f_b"][l])
        pcol[:, l, PC_NFIN:PC_NFIN + 8] = colmaj(inp["norm_final"])
    pcol = pcol.reshape(128, L * NPC)
    prow = np.zeros((L, NPR), f)
    for l in range(L):
        prow[l, 0:4] = inp["a_log"][l]
        prow[l, 4:8] = inp["dt_bias"][l]
        prow[l, 8:264] = inp["b_gate_b"][l]
    prow = prow.reshape(1, L * NPR)
    return dict(wst=wst, wkv=wkv, wsm=wsm, w2=np.ascontiguousarray(w2), pcol=np.ascontiguousarray(pcol), prow=prow, cst=_consts())


def _core_inputs(inp, shared, c, cfg):
    L, NP = cfg.L, cfg.NP
    f = np.float32
    m = dict(shared)
    m["xp"] = np.ascontiguousarray(np.asarray(inp["x_prompt"], f)[c * NP:(c + 1) * NP])
    m["memp"] = np.ascontiguousarray(np.asarray(inp["mem_prompt"], f)[c * NP:(c + 1) * NP])
    if cfg.sample:
        m["xs"] = np.ascontiguousarray(np.asarray(inp["x_sample"], f)[c])
        m["sgdn"] = np.ascontiguousarray(np.asarray(inp["state_gdn"], f)[:L, c])
        gc = np.asarray(inp["state_gdn_conv"], f)[:L, c]
        m["sgdc"] = np.ascontiguousarray(gc.reshape(L, 3, 12, 128).transpose(0, 3, 2, 1).reshape(L, 128, 36))
        m["sgla"] = np.ascontiguousarray(np.asarray(inp["state_gla"], f)[:L, c])
        fc = np.asarray(inp["state_ffn_conv"], f)[:L, c]
        m["sffc"] = np.ascontiguousarray(fc.reshape(L, 2, 44, 128).transpose(0, 3, 2, 1).reshape(L, 128, 88))
        m["cmk"] = np.ascontiguousarray(np.asarray(inp["cache_mem_k"], f)[:L, c].reshape(L, NMEM, D))
        m["cmv"] = np.ascontiguousarray(np.asarray(inp["cache_mem_v"], f)[:L, c].reshape(L, NMEM, D))
    return m


_CACHE = {}


def run(inp, cfg, ncores):
    key = (cfg.NP, cfg.SEQ, cfg.L, cfg.sample, cfg.TT, cfg.stop)
    if key not in _CACHE:
        _CACHE[key] = build(cfg)[0]
    nc = _CACHE[key]
    shared = _prep_weights(inp, cfg.L)
    in_maps = [_core_inputs(inp, shared, c, cfg) for c in range(ncores)]
    res = run_bass_kernel_spmd(nc, in_maps, core_ids=list(range(ncores)))
    return res.results


def kernel(**inp):
    cfg = Cfg()
    ncores = 8
    r = run(inp, cfg, ncores)
    L, NP = cfg.L, cfg.NP
    cat = np.concatenate
    y_prompt = cat([r[c]["yp"] for c in range(ncores)], axis=0)
    y_sample = np.stack([r[c]["ys"] for c in range(ncores)], axis=0)

    def split(name, shape_tail):
        p = cat([r[c][name][:, :NP] for c in range(ncores)], axis=1)
        s = cat([r[c][name][:, NP:NP + 1] for c in range(ncores)], axis=1)
        return p, s

    gdn_p, gdn_s = split("o_gdn", None)
    gdc_p, gdc_s = split("o_gdc", None)
    gla_p, gla_s = split("o_gla", None)
    ffc_p, ffc_s = split("o_ffc", None)
    mk = cat([r[c]["o_mk"] for c in range(ncores)], axis=1).reshape(L, 8 * NP, NMEM, 4, 256)
    mv = cat([r[c]["o_mv"] for c in range(ncores)], axis=1).reshape(L, 8 * NP, NMEM, 4, 256)
    outs = (y_prompt, y_sample, gdn_p, gdc_p, gla_p, ffc_p, mk, mv, gdn_s, gdc_s, gla_s, ffc_s)
    return tuple(np.ascontiguousarray(o, dtype=np.float32) for o in outs)
```

```python
import bisect
import contextlib
import numpy as np
import concourse.bass as bass
import concourse.mybir as mybir
from concourse.bass_utils import run_bass_kernel_spmd

F32 = mybir.dt.float32
BF16 = mybir.dt.bfloat16
AF = mybir.ActivationFunctionType
ALU = mybir.AluOpType

D = 1024
KC = 8
L_FULL = 4
SEQ_FULL = 2048
NMEM = 256
DFF = 2816
NIN = 5656
EPS = 1e-6
NU = 38
UNIT = 4096
NSLOT = 3
SEMLIM = 30000
BIG = 30000.0

PC_NMIX, PC_NMEM, PC_NFFN, PC_NKV, PC_CONVA, PC_ONA, PC_ONB, PC_CFW, PC_CFB, PC_NFIN = 0, 8, 16, 24, 32, 80, 81, 82, 214, 258
NPC = 266
NPR = 264


def _esize(dt):
    return 2 if dt == BF16 else 4


class Op:
    __slots__ = ("eng", "fn", "deps", "sig", "sem", "val", "dma", "n")

    def __init__(self, eng, fn, dma):
        self.eng, self.fn, self.dma = eng, fn, dma
        self.deps = {}
        self.sig = False
        self.sem = None
        self.val = 0


class Prog:
    ENGS = ("pe", "act", "dve", "pool", "sp")

    def __init__(self):
        self.ops = {e: [] for e in self.ENGS}
        self.segs = {}
        self.nops = 0

    @staticmethod
    def region(ap):
        t = ap.tensor
        es = _esize(ap.dtype)
        dims = ap.ap
        if type(t).__name__ == "DRamTensorHandle":
            lo = ap.offset
            ext = sum((c - 1) * abs(s) for s, c in dims) + 1
        elif type(t).__name__ == "PSumTensorHandle":
            return t.name, 0, 2048
        else:
            pstep = dims[0][0]
            lo = ap.offset % pstep if pstep else ap.offset
            ext = sum((c - 1) * abs(s) for s, c in dims[1:]) + 1
        return t.name, lo * es, (lo + ext) * es

    def _touch(self, op, ap, write):
        name, lo, hi = self.region(ap)
        st = self.segs.setdefault(name, [[], []])
        starts, segs = st
        i = bisect.bisect_right(starts, lo) - 1
        if i >= 0 and segs[i][1] > lo and segs[i][0] < lo:
            s = segs[i]
            ns = [lo, s[1], s[2], dict(s[3])]
            s[1] = lo
            segs.insert(i + 1, ns)
            starts.insert(i + 1, lo)
        i = bisect.bisect_left(starts, lo)
        cur = lo
        out = []
        while cur < hi:
            if i < len(segs) and segs[i][0] == cur:
                s = segs[i]
                if s[1] > hi:
                    ns = [hi, s[1], s[2], dict(s[3])]
                    s[1] = hi
                    segs.insert(i + 1, ns)
                    starts.insert(i + 1, hi)
                out.append(s)
                cur = s[1]
                i += 1
            else:
                nxt = segs[i][0] if i < len(segs) else hi
                nxt = min(nxt, hi)
                s = [cur, nxt, None, {}]
                segs.insert(i, s)
                starts.insert(i, cur)
                out.append(s)
                cur = nxt
                i += 1
        for s in out:
            if s[2] is not None and s[2] is not op:
                op.deps[id(s[2])] = s[2]
            if write:
                for r in s[3].values():
                    if r is not op:
                        op.deps[id(r)] = r
        if write:
            first = out[0]
            i0 = bisect.bisect_left(starts, first[0])
            del segs[i0:i0 + len(out)]
            del starts[i0:i0 + len(out)]
            segs.insert(i0, [lo, hi, op, {}])
            starts.insert(i0, lo)
        else:
            for s in out:
                key = op.eng if not op.dma else ("dma", id(op))
                s[3][key] = op

    def add(self, eng, fn, reads=(), writes=(), dma=False):
        op = Op(eng, fn, dma)
        op.n = self.nops
        self.nops += 1
        for ap in reads:
            self._touch(op, ap, type(ap.tensor).__name__ == "PSumTensorHandle")
        for ap in writes:
            self._touch(op, ap, True)
        self.ops[eng].append(op)
        return op

    def emit(self, nc, stack):
        for e in self.ENGS:
            for op in self.ops[e]:
                for d in op.deps.values():
                    if d.eng == "pe" and op.eng == "pe" and not d.dma:
                        continue
                    d.sig = True
        final_waits = {}
        for e in self.ENGS:
            cnt = 0
            sems = []
            for op in self.ops[e]:
                if op.dma or not op.sig:
                    continue
                k = cnt // SEMLIM
                if k >= len(sems):
                    sems.append(stack.enter_context(nc.semaphore("s_%s_%d" % (e, k))))
                op.sem = sems[k]
                op.val = cnt % SEMLIM + 1
                cnt += 1
        dma_prev = {}
        for e in self.ENGS:
            pool = None
            i = 0
            for op in self.ops[e]:
                if not op.dma:
                    continue
                if pool is None:
                    pool = [[stack.enter_context(nc.semaphore("d_%s_%d" % (e, j))), 0, None] for j in range(12)]
                slot = pool[i % len(pool)]
                i += 1
                slot[1] += 16
                op.sem = slot[0]
                op.val = slot[1]
                op.sig = True
                if slot[2] is not None:
                    op.deps[id(slot[2])] = slot[2]
                slot[2] = op
                final_waits[id(op.sem)] = (op.sem, op.val)
        engobj = {"pe": "tensor", "act": "scalar", "dve": "vector", "pool": "gpsimd", "sp": "sync"}
        prog = self

        def run(ename, e):
            seen = {}
            for op in prog.ops[ename]:
                for d in op.deps.values():
                    if d.eng == "pe" and ename == "pe" and not d.dma:
                        continue
                    k = id(d.sem)
                    if seen.get(k, 0) >= d.val:
                        continue
                    seen[k] = d.val
                    e.wait_ge(d.sem, d.val)
                ins = op.fn(e)
                if op.sig:
                    ins.then_inc(op.sem, 16 if op.dma else 1)
            if ename == "sp":
                for sem, val in final_waits.values():
                    e.wait_ge(sem, val)

        with nc.Block() as block:
            @block.sync
            def _(e):
                run("sp", e)

            @block.tensor
            def _(e):
                run("pe", e)

            @block.scalar
            def _(e):
                run("act", e)

            @block.vector
            def _(e):
                run("dve", e)

            @block.gpsimd
            def _(e):
                run("pool", e)


class StopBuild(Exception):
    pass


class Cfg:
    def __init__(self, NP=2, SEQ=SEQ_FULL, L=L_FULL, sample=True, TT=512, stop=None):
        self.NP, self.SEQ, self.L, self.sample, self.TT, self.stop = NP, SEQ, L, sample, TT, stop


def build(cfg):
    NP, SEQ, L = cfg.NP, cfg.SEQ, cfg.L
    TT = min(cfg.TT, SEQ)
    nc = bass.Bass("TRN2", target_bir_lowering=False)
    P = Prog()
    ES = contextlib.ExitStack()

    def din(name, shape, dt=F32):
        return nc.dram_tensor(name, list(shape), dt, kind="ExternalInput").ap()

    def dout(name, shape):
        return nc.dram_tensor(name, list(shape), F32, kind="ExternalOutput").ap()

    def dscr(name, shape, dt=BF16):
        return nc.dram_tensor(name, list(shape), dt).ap()

    NS = 1 if cfg.sample else 0
    xp = din("xp", [NP, SEQ, D])
    memp = din("memp", [NP, NMEM, D])
    wst = din("wst", [L, NU, 128, UNIT])
    wkv = din("wkv", [L, 4, 128, UNIT])
    wsm = din("wsm", [L, 128, KC * 24])
    w2 = din("w2", [16, L * 256])
    pcol = din("pcol", [128, L * NPC])
    prow = din("prow", [1, L * NPR])
    cst = din("cst", [128, 640])
    if NS:
        xs = din("xs", [64, D])
        sgdn = din("sgdn", [L, 4, 128, 128])
        sgdc = din("sgdc", [L, 128, 36])
        sgla = din("sgla", [L, 4, 64, 128])
        sffc = din("sffc", [L, 128, 88])
        cmk = din("cmk", [L, NMEM, D])
        cmv = din("cmv", [L, NMEM, D])
    yp = dout("yp", [NP, SEQ, D])
    o_gdn = dout("o_gdn", [L, NP + NS, 4, 128, 128])
    o_gdc = dout("o_gdc", [L, NP + NS, 3, 1536])
    o_gla = dout("o_gla", [L, NP + NS, 4, 64, 128])
    o_ffc = dout("o_ffc", [L, NP + NS, 2, 2 * DFF])
    o_mk = dout("o_mk", [L, NP, NMEM, D])
    o_mv = dout("o_mv", [L, NP, NMEM, D])
    if NS:
        ys = dout("ys", [64, D])
    wbf = dscr("wbf", [L, NU, 128, UNIT])
    wkvbf = dscr("wkvbf", [L, 4, 128, UNIT])
    kts = dscr("kts", [L, 128, KC * NMEM])
    vsc = dscr("vsc", [L, 128, 2 * D])

    def sb(name, shape, dt=F32):
        return ES.enter_context(nc.sbuf_tensor(name, list(shape), dt))

    xT = sb("xT", [128, KC, TT])
    xn = sb("xn", [128, KC, TT], BF16)
    wring = [sb("wr%d" % i, [128, UNIT], BF16) for i in range(NSLOT)]
    Sg = sb("Sg", [128, L, 4, 128])
    Sgb = sb("Sgb", [128, L, 4, 128], BF16)
    Sl = sb("Sl", [128, L, 2, 128])
    Slb = sb("Slb", [128, L, 2, 128], BF16)
    car_a = sb("car_a", [128, L, 12, 3])
    car_f = sb("car_f", [128, L, 44, 2])
    cs = sb("cs", [128, 640])
    identf = cs[:, 0:128]
    triU = cs[:, 128:256]
    NEGT = cs[:, 256:384]
    POSL = cs[:, 384:512]
    onesf = cs[:, 512:640]
    csb = sb("csb", [128, 384], BF16)
    identb = csb[:, 0:128]
    triUb = csb[:, 128:256]
    onesb = csb[:, 256:384]
    pc = sb("pc", [128, L * NPC])
    pr = sb("pr", [128, L * NPR])
    nega = sb("nega", [128, L * 4])
    wsmb = sb("wsmb", [128, L, KC * 24], BF16)
    w2b = sb("w2b", [16, L * 256], BF16)
    AR_BYTES = 124 * 1024
    AR = sb("AR", [128, AR_BYTES // 4])
    banks = [ES.enter_context(nc.psum_tensor("ps%d" % i, [128, 512], F32)) for i in range(8)]
    bank_i = [0]

    def bank():
        b = banks[bank_i[0] % 8]
        bank_i[0] += 1
        return b

    ar_top = [0]
    pf_live = [False]

    def aalloc(shape, dt=F32):
        n = 1
        for s_ in shape[1:]:
            n *= s_
        nb = n * _esize(dt)
        nb = (nb + 63) // 64 * 64
        off = ar_top[0]
        ar_top[0] += nb
        assert ar_top[0] <= AR_BYTES, ("arena overflow", ar_top[0])
        assert not (pf_live[0] and ar_top[0] > AR_BYTES - 16384), ("prefetch region clobbered", ar_top[0])
        v = AR[:, off // 4:(off + nb) // 4]
        if dt == BF16:
            v = v.bitcast(BF16)
        v = v[:, 0:n]
        if len(shape) == 3:
            v = v.rearrange("p (a b) -> p a b", b=shape[2])
        elif len(shape) == 4:
            v = v.rearrange("p (a b c) -> p a b c", b=shape[2], c=shape[3])
        return v

    def ckpt(name):
        if cfg.stop == name:
            raise StopBuild()

    def isap(a):
        return not isinstance(a, (int, float)) and a is not None

    def tt(eng, out, in0, in1, op):
        P.add(eng, lambda e: e.tensor_tensor(out=out, in0=in0, in1=in1, op=op), [in0, in1], [out])

    def ts(eng, out, in0, s1, s2=None, op0=ALU.mult, op1=None):
        rd = [in0] + [s for s in (s1, s2) if isap(s)]
        if s2 is None:
            P.add(eng, lambda e: e.tensor_scalar(out=out, in0=in0, scalar1=s1, scalar2=None, op0=op0), rd, [out])
        else:
            P.add(eng, lambda e: e.tensor_scalar(out=out, in0=in0, scalar1=s1, scalar2=s2, op0=op0, op1=op1), rd, [out])

    def stt(eng, out, in0, scalar, in1, op0, op1):
        rd = [in0, in1] + ([scalar] if isap(scalar) else [])
        eng = "dve"
        P.add(eng, lambda e: e.scalar_tensor_tensor(out=out, in0=in0, scalar=scalar, in1=in1, op0=op0, op1=op1), rd, [out])

    def cp(eng, out, in_):
        if eng == "act":
            P.add("act", lambda e: e.activation(out=out, in_=in_, func=AF.Copy), [in_], [out])
        else:
            P.add(eng, lambda e: e.tensor_copy(out=out, in_=in_), [in_], [out])

    def act(out, in_, func, bias=None, scale=None, accum=None):
        rd = [in_] + [s for s in (bias, scale) if isap(s)]
        wr = [out] + ([accum] if accum is not None else [])
        kw = {}
        if bias is not None:
            kw["bias"] = bias
        if scale is not None:
            kw["scale"] = scale
        if accum is not None:
            kw["accum_out"] = accum
        P.add("act", lambda e: e.activation(out=out, in_=in_, func=func, **kw), rd, wr)

    def mm(out, lhsT, rhs, start=True, stop=True):
        P.add("pe", lambda e: e.matmul(out, lhsT, rhs, start=start, stop=stop), [lhsT, rhs], [out])

    def tr(out, in_, ident):
        P.add("pe", lambda e: e.transpose(out, in_, ident), [in_, ident], [out])

    def dma(q, out, in_, **kw):
        P.add(q, lambda e: e.dma_start(out=out, in_=in_, **kw), [in_], [out], dma=True)

    def memset(eng, ap, v):
        P.add(eng, lambda e: e.memset(ap, v), [], [ap])

    def recip(out, in_):
        P.add("dve", lambda e: e.reciprocal(out=out, in_=in_), [in_], [out])

    def rmax(out, in_):
        P.add("dve", lambda e: e.reduce_max(out=out, in_=in_, axis=mybir.AxisListType.X), [in_], [out])

    dma("sp", cs[:, :], cst[:, :])
    dma("sp", pc[:, :], pcol[:, :])
    dma("sp", pr[:, :], prow[0:1, :].partition_broadcast(128))
    cp("dve", csb[:, 0:256], cs[:, 0:256])
    cp("dve", csb[:, 256:384], cs[:, 512:640])
    dma("pool", w2b[:, :], w2[:, :])
    for l in range(L):
        dma("pool", wsmb[:, l, :], wsm[l, :, :])
        act(nega[:, l * 4:l * 4 + 4], pr[:, l * NPR:l * NPR + 4], AF.Exp)
    ts("dve", nega[:, :], nega[:, :], -1.0)

    def cast_units(dst, src, l, u0, u1):
        o = dst[l, u0:u1].rearrange("u p (a b) -> (u p a) b", b=2048)
        i = src[l, u0:u1].rearrange("u p (a b) -> (u p a) b", b=2048)
        dma("pool", o, i)

    def cast_layer(l):
        for u0 in range(0, NU, 4):
            cast_units(wbf, wst, l, u0, min(NU, u0 + 4))

    if NP:
        cast_units(wkvbf, wkv, 0, 0, 4)
    cast_layer(0)
    if NP:
        for l in range(1, L):
            cast_units(wkvbf, wkv, l, 0, 4)
    cast_state = {"next": 1}

    seqs = [("p", i) for i in range(NP)] + ([("s", 0)] if NS else [])
    sched = []
    for kind, si in seqs:
        if kind == "p":
            for l in range(L):
                for u in range(4):
                    sched.append(wkvbf[l, u])
        nt = (SEQ // TT) if kind == "p" else 1
        for t in range(nt):
            for l in range(L):
                for u in range(NU):
                    sched.append(wbf[l, u])
    wstate = {"issued": 0, "used": 0}

    def w_issue():
        i = wstate["issued"]
        if i < len(sched):
            dma("sp", wring[i % NSLOT][:, :], sched[i])
            wstate["issued"] += 1

    def wnext():
        while wstate["issued"] < min(len(sched), wstate["used"] + NSLOT):
            w_issue()
        i = wstate["used"]
        wstate["used"] += 1
        return wring[i % NSLOT]

    def rmsnorm(src, gcol0, T, dst, kc=KC, mark=None):
        m0 = ar_top[0]
        sq = aalloc([128, 2, T], BF16)
        rs = aalloc([128, T])
        ps = bank()
        for j in range(kc):
            s_ = sq[:, j % 2, :]
            act(s_, src[:, j, 0:T], AF.Square)
            mm(ps[:, 0:T], onesb, s_, start=(j == 0), stop=(j == kc - 1))
        act(rs, ps[:, 0:T], AF.Ln, bias=EPS, scale=1.0 / (kc * 128))
        act(rs, rs, AF.Exp, scale=-0.5)
        for j in range(kc):
            stt("dve" if j % 2 == 0 else "pool", dst[:, j, 0:T], src[:, j, 0:T], pc[:, gcol0 + j:gcol0 + j + 1], rs, ALU.mult, ALU.mult)
        ar_top[0] = m0

    def proj(n_units, kc, cols, rhs, T, evac, kouter=False):
        for u in range(n_units):
            w = wnext()
            nmc = cols // 128
            if u == 0 and kouter:
                pss = [bank() for _ in range(nmc)]
                for k in range(kc):
                    for mc in range(nmc):
                        mm(pss[mc][:, 0:T], w[:, k * cols + mc * 128:k * cols + mc * 128 + 128], rhs(k), start=(k == 0), stop=(k == kc - 1))
                for mc in range(nmc):
                    evac(u * nmc + mc, pss[mc][:, 0:T], w)
                continue
            for mc in range(nmc):
                ps = bank()
                for k in range(kc):
                    mm(ps[:, 0:T], w[:, k * cols + mc * 128:k * cols + mc * 128 + 128], rhs(k), start=(k == 0), stop=(k == kc - 1))
                evac(u * nmc + mc, ps[:, 0:T], w)

    PF = AR[:, (AR_BYTES - 16384) // 4:AR_BYTES // 4].rearrange("p (b d) -> p b d", d=D)

    def prefetch_tile(src_rows, T):
        nb = (T + 127) // 128
        rows = min(T, 128)
        dma("sp", PF[0:rows, 0:nb, :], src_rows.rearrange("(b p) d -> p b d", p=rows))
        pf_live[0] = True

    def load_tile_T(src_rows, T, dstT, q="pool", pre=False):
        m0 = ar_top[0]
        nb = (T + 127) // 128
        rows = min(T, 128)
        if pre:
            stg = PF
            pf_live[0] = False
        else:
            stg = aalloc([128, nb, D])
            dma(q, stg[0:rows, :, :], src_rows.rearrange("(b p) d -> p b d", p=rows))
        for j in range(KC):
            ps = bank()
            for b in range(nb):
                tr(ps[:, b * rows:(b + 1) * rows], stg[0:rows, b, j * 128:(j + 1) * 128], identf[0:rows, 0:rows])
            cp("act" if j % 2 == 0 else "dve", dstT[:, j, 0:T], ps[:, 0:T])
        ar_top[0] = m0

    def store_tile_T(srcT, T, dst_rows):
        m0 = ar_top[0]
        nb = (T + 127) // 128
        rows = min(T, 128)
        stg = aalloc([128, nb, D])
        for b in range(nb):
            for half in range(2):
                ps = bank()
                for jj in range(4):
                    j = half * 4 + jj
                    tr(ps[0:rows, jj * 128:(jj + 1) * 128], srcT[:, j, b * rows:(b + 1) * rows], identf)
                cp("act" if half == 0 else "dve", stg[0:rows, b, half * 512:(half + 1) * 512], ps[0:rows, :])
        dma("pool", dst_rows.rearrange("(b p) d -> p b d", p=rows), stg[0:rows, :, :])
        ar_top[0] = m0

    def mem_pass(si):
        m0 = ar_top[0]
        memT = aalloc([128, KC, NMEM])
        load_tile_T(memp[si], NMEM, memT, q=("sp" if si == 0 else "pool"))
        sq = aalloc([128, 2, NMEM], BF16)
        rs = aalloc([128, NMEM])
        ps = bank()
        for j in range(KC):
            act(sq[:, j % 2, :], memT[:, j, :], AF.Square)
            mm(ps[:, 0:NMEM], onesb, sq[:, j % 2, :], start=(j == 0), stop=(j == KC - 1))
        act(rs, ps[:, 0:NMEM], AF.Ln, bias=EPS, scale=1.0 / D)
        act(rs, rs, AF.Exp, scale=-0.5)
        mn = aalloc([128, KC, NMEM], BF16)
        ktb = aalloc([128, KC, NMEM], BF16)
        vb = aalloc([128, 2, D], BF16)
        stg = aalloc([128, 2, D])
        for l in range(L):
            for j in range(KC):
                g = pc[:, l * NPC + PC_NKV + j:l * NPC + PC_NKV + j + 1]
                stt("dve" if j % 2 == 0 else "pool", mn[:, j, :], memT[:, j, :], g, rs, ALU.mult, ALU.mult)
            for which in range(2):
                for half in range(2):
                    w = wnext()
                    if which == 0:
                        for mc in range(4):
                            ps = bank()
                            for k in range(KC):
                                mm(ps[:, 0:NMEM], w[:, k * 512 + mc * 128:k * 512 + mc * 128 + 128], mn[:, k, :], start=(k == 0), stop=(k == KC - 1))
                            cp("act", ktb[:, half * 4 + mc, :], ps[:, 0:NMEM])
                    for blk in range(2):
                        ps = bank()
                        for k in range(KC):
                            mm(ps[:, :], mn[:, k, blk * 128:(blk + 1) * 128], w[:, k * 512:(k + 1) * 512], start=(k == 0), stop=(k == KC - 1))
                        cp("dve", stg[:, blk, half * 512:(half + 1) * 512], ps[:, :])
                        if which == 1:
                            cp("act", vb[:, blk, half * 512:(half + 1) * 512], ps[:, :])
                dst = (o_mk if which == 0 else o_mv)[l, si]
                dma("pool", dst.rearrange("(b p) d -> p b d", p=128), stg[:, :, :])
            dma("pool", kts[l].rearrange("p (a b) -> p a b", b=NMEM), ktb[:, :, :])
            dma("pool", vsc[l].rearrange("p (a b) -> p a b", b=D), vb[:, :, :])
        ar_top[0] = m0

    def sample_kv_pass():
        m0 = ar_top[0]
        stg = aalloc([128, 2, D])
        ktb = aalloc([128, KC, NMEM], BF16)
        for l in range(L):
            dma("pool", stg[:, :, :], cmk[l].rearrange("(b p) d -> p b d", p=128))
            for j in range(KC):
                ps = bank()
                for b in range(2):
                    tr(ps[:, b * 128:(b + 1) * 128], stg[:, b, j * 128:(j + 1) * 128], identf)
                cp("act" if j % 2 == 0 else "dve", ktb[:, j, :], ps[:, 0:NMEM])
            dma("pool", kts[l].rearrange("p (a b) -> p a b", b=NMEM), ktb[:, :, :])
            dma("pool", vsc[l].rearrange("p (b d) -> p b d", d=D), cmv[l].rearrange("(b p) d -> p b d", p=128))
        ar_top[0] = m0

    def layer(l, T, last_tile, oslot, ffn_hook=None):
        C = min(128, T)
        NCH = T // C
        pcl = l * NPC
        prl = l * NPR
        m_layer = ar_top[0]
        rmsnorm(xT, pcl + PC_NMIX, T, xn)
        qT = aalloc([128, 4, T], BF16)
        kT = aalloc([128, 4, T], BF16)
        vT = aalloc([128, 4, T], BF16)
        zaT = aalloc([128, 4, T], BF16)
        zbT = aalloc([128, 4, T], BF16)
        qkb = aalloc([128, 4, T])
        vbT = aalloc([128, 4, T], BF16)
        glrT = aalloc([128, T], BF16)
        sp_ = aalloc([128, NCH, 256])
        bet = aalloc([128, NCH, 4])
        gg = aalloc([128, NCH, 4])
        oaT = aalloc([128, 4, T], BF16)
        obT = aalloc([128, 4, T], BF16)
        mrg = aalloc([128, KC, T], BF16)
        sgb = aalloc([128, KC, T], BF16)
        tg = aalloc([128, 2, T], BF16)
        m_tmp = ar_top[0]
        raw = aalloc([128, 4, 3 + T])
        cv = aalloc([128, 4, T])
        sq = aalloc([128, 2, T], BF16)
        rs = aalloc([128, 2, T])
        tmp4 = aalloc([128, 8])

        def rhs_xn(k):
            return xn[:, k, 0:T]

        sl8 = aalloc([128, 8, T])

        def stageB(j):
            s_ = sl8[:, j, :]
            q_ = sq[:, j % 2, :]
            tt("pool", q_, s_, s_, ALU.mult)
            ps2 = bank()
            mm(ps2[:, 0:T], onesb, q_)
            r_ = rs[:, j % 2, :]
            act(r_, ps2[:, 0:T], AF.Ln, bias=EPS)
            act(r_, r_, AF.Exp, scale=-0.5)
            if j < 4:
                stt("dve", qT[:, j, :], s_, 128.0 ** -0.5, r_, ALU.mult, ALU.mult)
            else:
                tt("dve", kT[:, j - 4, :], s_, r_, ALU.mult)

        def evac_qkv(j, ps, w):
            r = raw[:, j % 4, :]
            cw = pcl + PC_CONVA + j * 4
            c_ = cv[:, j % 4, :]
            cp("pool", r[:, 0:3], car_a[:, l, j, :])
            cp("act", r[:, 3:3 + T], ps)
            act(c_, ps, AF.Identity, scale=pc[:, cw + 3:cw + 4])
            cp("pool", car_a[:, l, j, :], r[:, T:T + 3])
            for i in range(3):
                stt("dve", c_, r[:, i:i + T], pc[:, cw + i:cw + i + 1], c_, ALU.mult, ALU.add)
            def tail(j=j, c_=c_):
                if j >= 8:
                    act(vT[:, j - 8, :], c_, AF.Silu)
                else:
                    act(sl8[:, j, :], c_, AF.Silu)
            pend_q.append(tail)
            while len(pend_q) > 2:
                pend_q.pop(0)()
            if j == 5:
                for jj in range(0, 4):
                    stageB(jj)
            if j == 9:
                for jj in range(4, 8):
                    stageB(jj)

        if last_tile:
            stg3 = aalloc([128, 1536])
        proj_state = {}

        def evac_qkv_w(j, ps, w):
            evac_qkv(j, ps, w)
            if last_tile and j % 4 == 3:
                u = j // 4
                ps3 = bank()
                for k in range(KC):
                    mm(ps3[0:3, :], xn[:, k, T - 3:T], w[:, k * 512:(k + 1) * 512], start=(k == 0), stop=(k == KC - 1))
                cp("act", stg3[0:3, u * 512:(u + 1) * 512], ps3[0:3, :])
                if u == 2:
                    dma("pool", o_gdc[l, oslot], stg3[0:3, :])

        pend_q = []
        proj(3, KC, 512, rhs_xn, T, evac_qkv_w, kouter=True)
        while pend_q:
            pend_q.pop(0)()

        ps = bank()
        for k in range(KC):
            mm(ps[0:16, 0:T], wsmb[:, l, k * 24:k * 24 + 16], xn[:, k, 0:T], start=(k == 0), stop=(k == KC - 1))
        cp("act", glrT[0:16, :], ps[0:16, 0:T])
        for c in range(NCH):
            ps = bank()
            for k in range(KC):
                mm(ps[0:C, 0:8], xn[:, k, c * C:(c + 1) * C], wsmb[:, l, k * 24 + 16:k * 24 + 24], start=(k == 0), stop=(k == KC - 1))
            act(tmp4[0:C, 0:4], ps[0:C, 0:4], AF.Exp, scale=-1.0)
            ts("dve", tmp4[0:C, 0:4], tmp4[0:C, 0:4], 1.0, None, ALU.add)
            recip(bet[0:C, c, :], tmp4[0:C, 0:4])
            tt("dve", tmp4[0:C, 4:8], ps[0:C, 4:8], pr[0:C, prl + 4:prl + 8], ALU.add)
            act(tmp4[0:C, 4:8], tmp4[0:C, 4:8], AF.Exp)
            act(tmp4[0:C, 4:8], tmp4[0:C, 4:8], AF.Ln, bias=1.0)
            tt("dve", gg[0:C, c, :], tmp4[0:C, 4:8], nega[0:C, l * 4:l * 4 + 4], ALU.mult)
            ps2 = bank()
            mm(ps2[0:C, 0:256], glrT[0:16, c * C:(c + 1) * C], w2b[:, l * 256:(l + 1) * 256])
            tt("dve", sp_[0:C, c, :], ps2[0:C, 0:256], pr[0:C, prl + 8:prl + 264], ALU.add)
            act(sp_[0:C, c, :], sp_[0:C, c, :], AF.Exp, scale=-1.0)
            act(sp_[0:C, c, :], sp_[0:C, c, :], AF.Ln, bias=1.0)

        def evac_silu(dst):
            def f(j, ps, w):
                act(dst[:, j, :], ps, AF.Silu)
            return f

        def evac_qkb(j, ps, w):
            cp("act" if j % 2 == 0 else "dve", qkb[:, j, :], ps)
        proj(1, KC, 512, rhs_xn, T, evac_qkb)

        def evac_vb(j, ps, w):
            cp("act" if j % 2 == 0 else "dve", vbT[:, j, :], ps)
        proj(1, KC, 512, rhs_xn, T, evac_vb)
        proj(1, KC, 512, rhs_xn, T, evac_silu(zaT))
        proj(1, KC, 512, rhs_xn, T, evac_silu(zbT))
        ar_top[0] = m_tmp
        ckpt('front')

        m_c = ar_top[0]
        Gcol = aalloc([128, 2, 4])
        gb = aalloc([128, 4, 128])
        EG = aalloc([128, 2, 4, C])
        DT = aalloc([128, 4, C])
        Dm = aalloc([128, 4, C])
        Psets = [[aalloc([128, 4, C]) for _ in range(5)] for _ in range(2)]
        TTb = aalloc([128, 2, 4, C], BF16)
        AqT = aalloc([128, 2, 4, C], BF16)
        QgT = aalloc([128, 2, 4, C], BF16)
        Vb = aalloc([128, 2, 4, 128])
        Kg = aalloc([128, 2, 4, 128], BF16)
        Rt = aalloc([128, 4, 128])
        Rb = aalloc([128, 4, 128], BF16)
        vnb = aalloc([128, 4, 128], BF16)
        sm = aalloc([128, 2, 16])
        nsteps = {128: 6, 64: 5}[C]
        st_ = {"pdone": set(), "c_done": 0, "early": 0, "z": 2, "fill": 8, "oa": 0, "ob": 0}
        sqa = aalloc([128, 4 * C], BF16)
        sqb = aalloc([128, 4 * C], BF16)
        rsb = aalloc([128, 4, C])

        rsa = aalloc([128, 4, C])

        def norm_stream(key, oT_, zT_, gc, sq_, rs_, zneed):
            for c in range(NCH):
                while st_[key] <= c or st_["z"] < zneed:
                    yield "blocked"
                ck = slice(c * C, (c + 1) * C)
                o3 = oT_[:, :, ck]
                act(sq_.rearrange("p (h c) -> p h c", c=C), o3, AF.Square)
                yield None
                psq = bank()
                mm(psq[:, 0:4 * C], onesb, sq_)
                act(rs_, psq[:, 0:4 * C].rearrange("p (h c) -> p h c", c=C), AF.Ln, bias=EPS, scale=1.0 / 128)
                act(rs_, rs_, AF.Exp, scale=-0.5)
                yield None
                stt("dve", o3, o3, pc[:, pcl + gc:pcl + gc + 1], rs_, ALU.mult, ALU.mult)
                tt("pool", o3, o3, zT_[:, :, ck], ALU.mult)
                yield None

        def gdn_prep(c):
            B = c % 2
            Pa, Pta, Pb, Ptb, X = Psets[B]
            t0 = c * C
            ck = slice(t0, t0 + C)
            ps = bank()
            mm(ps[0:C, 0:4], triU[0:C, 0:C], gg[0:C, c, :])
            cp("act", Gcol[0:C, B, :], ps[0:C, 0:4])
            cp("pool", gb[0:C, :, :], gg[0:C, c, :].unsqueeze(2).to_broadcast([C, 4, 128]))
            yield
            psG = bank()
            for h in range(4):
                mm(psG[:, h * C:(h + 1) * C], gb[0:C, h, :], triU[0:C, 0:C])
            psG3 = psG[:, 0:4 * C].rearrange("p (h c) -> p h c", c=C)
            act(EG[:, B, :, :], psG3, AF.Exp)
            for h in range(4):
                stt("dve", DT[0:C, h, :], psG3[0:C, h, :], Gcol[0:C, B, h:h + 1], NEGT[0:C, 0:C], ALU.subtract, ALU.add)
                stt("dve", Dm[0:C, h, :], psG3[0:C, h, :], Gcol[0:C, B, h:h + 1], POSL[0:C, 0:C], ALU.subtract, ALU.add)
            tt("dve", sm[0:C, B, 4:8], psG3[0:C, :, C - 1], Gcol[0:C, B, :], ALU.subtract)
            yield
            act(DT[0:C, :, :], DT[0:C, :, :], AF.Exp)
            act(Dm[0:C, :, :], Dm[0:C, :, :], AF.Exp, scale=-1.0)
            ts("dve", sm[0:C, B, 0:4], bet[0:C, c, :], -1.0)
            act(sm[0:C, B, 4:8], sm[0:C, B, 4:8], AF.Exp)
            act(sm[0:C, B, 8:12], Gcol[0:C, B, :], AF.Exp)
            tt("dve", sm[0:C, B, 8:12], sm[0:C, B, 8:12], bet[0:C, c, :], ALU.mult)
            yield
            psK = bank()
            for h in range(4):
                mm(psK[0:C, h * C:(h + 1) * C], kT[:, h, ck], kT[:, h, ck])
            for h in range(4):
                stt("dve", Pa[0:C, h, :], psK[0:C, h * C:(h + 1) * C], sm[0:C, B, h:h + 1], Dm[0:C, h, :], ALU.mult, ALU.mult)
            yield
            psT = bank()
            for h in range(4):
                tr(psT[0:C, h * C:(h + 1) * C], Pa[0:C, h, :], identf[0:C, 0:C])
            psT3 = psT[0:C, 0:4 * C].rearrange("p (h c) -> p h c", c=C)
            cp("act", Pta[0:C, :, :], psT3)
            tt("dve", X[0:C, :, :], psT3, identf[0:C, 0:C].unsqueeze(1).to_broadcast([C, 4, C]), ALU.add)
            yield
            psA = bank()
            for h in range(4):
                mm(psA[0:C, h * C:(h + 1) * C], kT[:, h, ck], qT[:, h, ck])
            tt("dve", AqT[0:C, B, :, :], psA[0:C, 0:4 * C].rearrange("p (h c) -> p h c", c=C), DT[0:C, :, :], ALU.mult)
            tt("pool", QgT[:, B, :, :], qT[:, :, ck], EG[:, B, :, :], ALU.mult)
            yield
            psV = bank()
            psVb = psV[:, :].bitcast(BF16)
            for h in range(4):
                tr(psVb[0:C, h * 128:(h + 1) * 128], vT[:, h, ck], identb)
            tt("dve", Vb[0:C, B, :, :], psVb[0:C, 0:512].rearrange("p (h d) -> p h d", d=128),
               bet[0:C, c, :].unsqueeze(2).to_broadcast([C, 4, 128]), ALU.mult)
            yield
            psKt = bank()
            psKb = psKt[:, :].bitcast(BF16)
            for h in range(4):
                tr(psKb[0:C, h * 128:(h + 1) * 128], kT[:, h, ck], identb)
            tt("dve", Kg[0:C, B, :, :], psKb[0:C, 0:512].rearrange("p (h d) -> p h d", d=128),
               sm[0:C, B, 4:8].unsqueeze(2).to_broadcast([C, 4, 128]), ALU.mult)
            st_["early"] = c + 1
            yield
            Pc, Ptc, Pn, Ptn = Pa, Pta, Pb, Ptb
            for st in range(nsteps):
                lastst = st == nsteps - 1
                ps1 = bank()
                for h in range(4):
                    mm(ps1[0:C, h * C:(h + 1) * C], Ptc[0:C, h, :], Pc[0:C, h, :])
                cp("act", Pn[0:C, :, :], ps1[0:C, 0:4 * C].rearrange("p (h c) -> p h c", c=C))
                if not lastst:
                    ps2 = bank()
                    for h in range(4):
                        mm(ps2[0:C, h * C:(h + 1) * C], Pc[0:C, h, :], Ptc[0:C, h, :])
                    cp("dve", Ptn[0:C, :, :], ps2[0:C, 0:4 * C].rearrange("p (h c) -> p h c", c=C))
                yield
                ps3 = bank()
                for h in range(4):
                    mm(ps3[0:C, h * C:(h + 1) * C], Pn[0:C, h, :], X[0:C, h, :])
                tt("dve", X[0:C, :, :], X[0:C, :, :], ps3[0:C, 0:4 * C].rearrange("p (h c) -> p h c", c=C), ALU.add)
                Pc, Ptc, Pn, Ptn = Pn, Ptn, Pc, Ptc
                yield
            cp("act", TTb[0:C, B, :, :], X[0:C, :, :])
            yield

        def gdn_chain(c):
            B = c % 2
            t0 = c * C
            ck = slice(t0, t0 + C)
            psP = bank()
            for h in range(4):
                mm(psP[0:C, h * 128:(h + 1) * 128], kT[:, h, ck], Sgb[:, l, h, :])
            tt("dve", Rt[0:C, :, :], psP[0:C, :].rearrange("p (h d) -> p h d", d=128),
               sm[0:C, B, 8:12].unsqueeze(2).to_broadcast([C, 4, 128]), ALU.mult)
            tt("pool", Rb[0:C, :, :], Vb[0:C, B, :, :], Rt[0:C, :, :], ALU.subtract)
            yield
            psN = bank()
            for h in range(4):
                mm(psN[0:C, h * 128:(h + 1) * 128], TTb[0:C, B, h, :], Rb[0:C, h, :])
            cp("act", vnb[0:C, :, :], psN[0:C, :].rearrange("p (h d) -> p h d", d=128))
            yield
            psO = bank()
            for h in range(4):
                mm(psO[:, h * C:(h + 1) * C], Sgb[:, l, h, :], QgT[:, B, h, :], start=True, stop=False)
                mm(psO[:, h * C:(h + 1) * C], vnb[0:C, h, :], AqT[0:C, B, h, :], start=False, stop=True)
            cp("act", oaT[:, :, ck], psO[:, 0:4 * C].rearrange("p (h c) -> p h c", c=C))
            st_["oa"] = c + 1
            yield
            psS = bank()
            for h in range(4):
                mm(psS[:, h * 128:(h + 1) * 128], Kg[0:C, B, h, :], vnb[0:C, h, :])
            tt("pool", Sg[:, l, :, :], Sg[:, l, :, :], EG[:, B, :, C - 1:C].to_broadcast([128, 4, 128]), ALU.mult)
            tt("dve", Sg[:, l, :, :], Sg[:, l, :, :], psS[:, :].rearrange("p (h d) -> p h d", d=128), ALU.add)
            cp("act", Sgb[:, l, :, :], Sg[:, l, :, :])
            yield

        eG = aalloc([128, 2, C])
        enG = aalloc([128, 2, C])
        ekl = aalloc([128, 2, C])
        glc = aalloc([128, 4])
        qgz = aalloc([128, 4, C], BF16)
        memset("pool", qgz[:, :, :], 0.0)
        kng = aalloc([128, 2, C], BF16)
        kg = aalloc([128, 2, C], BF16)
        KgT = aalloc([128, 2, 128], BF16)
        Vt = aalloc([128, 4, 128], BF16)
        ATb = aalloc([128, 4, C], BF16)

        def gla_chunk(c):
            t0 = c * C
            ck = slice(t0, t0 + C)
            psG = bank()
            for fc in range(2):
                mm(psG[:, fc * C:(fc + 1) * C], sp_[0:C, c, fc * 128:(fc + 1) * 128], triU[0:C, 0:C])
            psG3 = psG[:, 0:2 * C].rearrange("p (f c) -> p f c", c=C)
            act(eG[:, :, :], psG3, AF.Exp, scale=-1.0 / 16)
            act(enG[:, :, :], psG3, AF.Exp, scale=1.0 / 16)
            ts("dve", glc[:, 0:2], psG3[:, :, C - 1], -1.0 / 16)
            yield
            for fc in range(2):
                act(ekl[:, fc, :], psG3[:, fc, :], AF.Exp, scale=1.0 / 16, bias=glc[:, fc:fc + 1])
            act(glc[:, 2:4], glc[:, 0:2], AF.Exp)
            for h in range(4):
                fc, off = h // 2, 64 * (h % 2)
                stt("dve", qgz[off:off + 64, h, :], qkb[off:off + 64, fc, ck], 0.125, eG[off:off + 64, fc, :], ALU.mult, ALU.mult)
            tt("pool", kng[:, :, :], qkb[:, 2:4, ck], enG[:, :, :], ALU.mult)
            tt("pool", kg[:, :, :], qkb[:, 2:4, ck], ekl[:, :, :], ALU.mult)
            yield
            psK = bank()
            psKb = psK[:, :].bitcast(BF16)
            for fc in range(2):
                tr(psKb[0:C, fc * 128:(fc + 1) * 128], kg[:, fc, :], identb)
            cp("act", KgT[0:C, :, :], psKb[0:C, 0:256].rearrange("p (f d) -> p f d", d=128))
            yield
            psV = bank()
            psVb = psV[:, :].bitcast(BF16)
            for h in range(4):
                tr(psVb[0:C, h * 128:(h + 1) * 128], vbT[:, h, ck], identb)
            cp("dve", Vt[0:C, :, :], psVb[0:C, 0:512].rearrange("p (h d) -> p h d", d=128))
            yield
            psA = bank()
            for h in range(4):
                fc = h // 2
                mm(psA[0:C, h * C:(h + 1) * C], kng[:, fc, :], qgz[:, h, :])
            tt("dve", ATb[0:C, :, :], psA[0:C, 0:4 * C].rearrange("p (h c) -> p h c", c=C),
               triU[0:C, 0:C].unsqueeze(1).to_broadcast([C, 4, C]), ALU.mult)
            yield
            psO = bank()
            for h in range(4):
                fc = h // 2
                mm(psO[:, h * C:(h + 1) * C], Slb[:, l, fc, :], qgz[:, h, :], start=True, stop=False)
                mm(psO[:, h * C:(h + 1) * C], Vt[0:C, h, :], ATb[0:C, h, :], start=False, stop=True)
            cp("act", obT[:, :, ck], psO[:, 0:4 * C].rearrange("p (h c) -> p h c", c=C))
            st_["ob"] = c + 1
            yield
            psS = bank()
            for fc in range(2):
                mm(psS[:, fc * 256:(fc + 1) * 256], KgT[0:C, fc, :], Vt[0:C, 2 * fc:2 * fc + 2, :])
            for h in range(4):
                fc, off = h // 2, 64 * (h % 2)
                stt("dve", Sl[off:off + 64, l, fc, :], Sl[off:off + 64, l, fc, :], glc[off:off + 64, 2 + fc:3 + fc],
                    psS[off:off + 64, fc * 256 + (h % 2) * 128:fc * 256 + (h % 2) * 128 + 128], ALU.mult, ALU.add)
            cp("act", Slb[:, l, :, :], Sl[:, l, :, :])
            yield

        def prep_par(par):
            for c in range(par, NCH, 2):
                while c >= st_["c_done"] + 2 or st_["early"] < c:
                    yield "blocked"
                for _ in gdn_prep(c):
                    yield None
                st_["pdone"].add(c)

        def chain_all():
            for c in range(NCH):
                while c not in st_["pdone"]:
                    yield "blocked"
                for _ in gdn_chain(c):
                    yield None
                st_["c_done"] = c + 1

        def gla_all():
            for c in range(NCH):
                for _ in gla_chunk(c):
                    yield None

        def gates_all():
            for gi_, dst in enumerate((mrg, sgb)):
                for uu in range(2):
                    w = wnext()
                    for mc in range(4):
                        j = uu * 4 + mc
                        ps = bank()
                        for k in range(KC):
                            mm(ps[:, 0:T], w[:, k * 512 + mc * 128:k * 512 + mc * 128 + 128], xn[:, k, 0:T], start=(k == 0), stop=(k == KC - 1))
                        cp("act" if j % 2 == 0 else "dve", dst[:, j, :], ps[:, 0:T])
                        st_["fill"] += 1
                        yield None

        gens = [prep_par(0), prep_par(1), gla_all(), chain_all(), gates_all(),
                norm_stream("oa", oaT, zaT, PC_ONA, sqa, rsa[:, :, :], 1),
                norm_stream("ob", obT, zbT, PC_ONB, sqb, rsb[:, :, :], 2)]
        alive = [True] * 7
        rnd = 0
        while any(alive):
            rnd += 1
            for gi, g in enumerate(gens):
                if not alive[gi]:
                    continue
                if gi == 4 and st_["z"] >= 2 and (any(alive[0:4]) or any(alive[5:7])):
                    if any(alive[0:3]):
                        if rnd % 3 != 0 or st_["fill"] >= 16:
                            continue
                try:
                    next(g)
                    if gi == 4 and not any(alive[0:3]) and (alive[3] or alive[5] or alive[6]):
                        next(g)
                except StopIteration:
                    alive[gi] = False
        for dst in (mrg, sgb):
            for j in range(KC):
                act(dst[:, j, :], dst[:, j, :], AF.Tanh, scale=0.5)
                ts("dve" if j % 2 == 0 else "pool", dst[:, j, :], dst[:, j, :], 0.5, 0.5, ALU.mult, ALU.add)
        ar_top[0] = m_c
        ckpt('gla')


        def evac_ya(j, ps, w):
            tt("dve", mrg[:, j, :], mrg[:, j, :], ps, ALU.mult)
        proj(2, 4, 512, lambda k: oaT[:, k, :], T, evac_ya)

        def evac_yb(j, ps, w):
            tt("dve", sgb[:, j, :], sgb[:, j, :], ps, ALU.mult)
            tt("pool", mrg[:, j, :], mrg[:, j, :], sgb[:, j, :], ALU.add)
        proj(2, 4, 512, lambda k: obT[:, k, :], T, evac_yb)

        def evac_res(j, ps, w):
            tt("dve", xT[:, j, 0:T], xT[:, j, 0:T], ps, ALU.add)
        proj(2, KC, 512, lambda k: mrg[:, k, :], T, evac_res)
        ar_top[0] = m_layer
        ckpt('mixer')

        KTt = aalloc([128, KC, NMEM], BF16)
        Vtk = aalloc([128, 2, D], BF16)
        dma("pool", KTt[:, :, :], kts[l].rearrange("p (a b) -> p a b", b=NMEM))
        dma("pool", Vtk[:, :, :], vsc[l].rearrange("p (a b) -> p a b", b=D))
        rmsnorm(xT, pcl + PC_NMEM, T, xn)
        q2 = aalloc([128, KC, T], BF16)

        def evac_q2(j, ps, w):
            if j % 2 == 0:
                act(q2[:, j, :], ps, AF.Copy, scale=1.0 / 16)
            else:
                ts("dve", q2[:, j, :], ps, 1.0 / 16)
        proj(2, KC, 512, rhs_xn, T, evac_q2, kouter=True)
        oT2 = aalloc([128, KC, T], BF16)
        aT = aalloc([128, 2, 2, T], BF16)
        Ee = aalloc([128, 4, NMEM])
        ab = aalloc([128, 4, NMEM], BF16)
        st4 = aalloc([128, 16])
        NB = T // C
        items = [(h, b) for h in range(4) for b in range(NB)]

        def at_stage1(i):
            h, b = items[i]
            i4 = i % 4
            bk = slice(b * C, (b + 1) * C)
            ps = bank()
            for dc in range(2):
                mm(ps[0:C, 0:NMEM], q2[:, 2 * h + dc, bk], KTt[:, 2 * h + dc, :], start=(dc == 0), stop=(dc == 1))
            rmax(st4[0:C, i4:i4 + 1], ps[0:C, 0:NMEM])
            ts("dve", st4[0:C, 4 + i4:5 + i4], st4[0:C, i4:i4 + 1], -1.0)
            act(Ee[0:C, i4, :], ps[0:C, 0:NMEM], AF.Exp, bias=st4[0:C, 4 + i4:5 + i4], accum=st4[0:C, 8 + i4:9 + i4])
            recip(st4[0:C, 12 + i4:13 + i4], st4[0:C, 8 + i4:9 + i4])
            ts("dve", ab[0:C, i4, :], Ee[0:C, i4, :], st4[0:C, 12 + i4:13 + i4])

        def at_stage2(i):
            h, b = items[i]
            i4 = i % 4
            bk = slice(b * C, (b + 1) * C)
            psT = bank()
            psTb = psT[:, :].bitcast(BF16)
            for mc in range(2):
                tr(psTb[:, mc * C:(mc + 1) * C], ab[0:C, i4, mc * 128:(mc + 1) * 128], identb[0:C, 0:C])
            cp("act", aT[:, h % 2, :, bk], psTb[:, 0:2 * C].rearrange("p (m c) -> p m c", c=C))
            if b == NB - 1:
                for dc in range(2):
                    ps = bank()
                    for mc in range(2):
                        mm(ps[:, 0:T], Vtk[:, mc, (2 * h + dc) * 128:(2 * h + dc + 1) * 128], aT[:, h % 2, mc, :], start=(mc == 0), stop=(mc == 1))
                    cp("dve", oT2[:, 2 * h + dc, :], ps[:, 0:T])

        nit = len(items)
        for i in range(min(3, nit)):
            at_stage1(i)
        for i in range(nit):
            if i + 3 < nit:
                at_stage1(i + 3)
            at_stage2(i)
        proj(2, KC, 512, lambda k: oT2[:, k, :], T, evac_res)
        ar_top[0] = m_layer
        ckpt('attn')

        if ffn_hook is not None:
            ffn_hook()
        rmsnorm(xT, pcl + PC_NFFN, T, xn)
        hT = aalloc([128, 22, T], BF16)
        raw2 = aalloc([128, 8, 2 + T])
        cu = aalloc([128, 8, T])
        if last_tile:
            stg2 = aalloc([128, 2, 512])

        def evac_up(jj, ps, w):
            u, i = jj // 4, jj % 4
            ch = (2 * u + i) if i < 2 else (22 + 2 * u + (i - 2))
            i8 = (u % 2) * 4 + i
            r = raw2[:, i8, :]
            cp("pool", r[:, 0:2], car_f[:, l, ch, :])
            cp("act", r[:, 2:2 + T], ps)
            cp("pool", car_f[:, l, ch, :], r[:, T:T + 2])
            cw = pcl + PC_CFW + ch * 3
            cbb = pcl + PC_CFB + ch
            c_ = cu[:, i8, :]
            act(c_, ps, AF.Identity, bias=pc[:, cbb:cbb + 1], scale=pc[:, cw + 2:cw + 3])
            stt("dve", c_, r[:, 1:1 + T], pc[:, cw + 1:cw + 2], c_, ALU.mult, ALU.add)
            stt("dve", c_, r[:, 0:T], pc[:, cw:cw + 1], c_, ALU.mult, ALU.add)
            def tail(i=i, u=u, i8=i8, c_=c_):
                if i < 2:
                    act(c_, c_, AF.Silu)
                else:
                    tt("pool", hT[:, 2 * u + (i - 2), :], cu[:, i8 - 2, :], c_, ALU.mult)
            pend_f.append(tail)
            while len(pend_f) > 4:
                pend_f.pop(0)()
            if last_tile and i == 3:
                ps3 = bank()
                for k in range(KC):
                    mm(ps3[0:2, :], xn[:, k, T - 2:T], w[:, k * 512:(k + 1) * 512], start=(k == 0), stop=(k == KC - 1))
                sg_ = stg2[0:2, u % 2, :]
                cp("act", sg_, ps3[0:2, :])
                dma("pool", o_ffc[l, oslot, :, 256 * u:256 * u + 256], sg_[:, 0:256])
                dma("pool", o_ffc[l, oslot, :, DFF + 256 * u:DFF + 256 * u + 256], sg_[:, 256:512])

        pend_f = []
        proj(11, KC, 512, rhs_xn, T, evac_up, kouter=True)
        while pend_f:
            pend_f.pop(0)()
        for half in range(2):
            pss = [bank() for _ in range(4)]
            for kg_ in range(3):
                w = wnext()
                nk = 8 if kg_ < 2 else 6
                for mc in range(4):
                    for kk in range(nk):
                        k = kg_ * 8 + kk
                        mm(pss[mc][:, 0:T], w[:, kk * 512 + mc * 128:kk * 512 + mc * 128 + 128], hT[:, k, :], start=(k == 0), stop=(k == 21))
            for mc in range(4):
                evac_res(half * 4 + mc, pss[mc][:, 0:T], None)
        ar_top[0] = m_layer

    tile_list = []
    for kind_, si_ in seqs:
        for t_ in range((SEQ // TT) if kind_ == "p" else 1):
            tile_list.append((kind_, si_, t_))

    def main_loop():
        for kind, si in seqs:
            oslot = si if kind == "p" else NP
            T = TT if kind == "p" else 64
            nt = (SEQ // TT) if kind == "p" else 1
            if kind == "p":
                memset("pool", Sg[:, :, :, :], 0.0)
                memset("pool", Sgb[:, :, :, :], 0.0)
                memset("pool", Sl[:, :, :, :], 0.0)
                memset("pool", Slb[:, :, :, :], 0.0)
                memset("pool", car_a[:, :, :, :], 0.0)
                memset("pool", car_f[:, :, :, :], 0.0)
                mem_pass(si)
                ckpt('mem')
            else:
                for l in range(L):
                    dma("pool", Sg[:, l, :, :], sgdn[l].rearrange("h k v -> k h v"))
                    for fc in range(2):
                        dma("pool", Sl[:, l, fc, :], sgla[l, 2 * fc:2 * fc + 2].rearrange("h d v -> (h d) v"))
                    dma("pool", car_a[:, l, :, :], sgdc[l].rearrange("p (a b) -> p a b", b=3))
                    dma("pool", car_f[:, l, :, :], sffc[l].rearrange("p (a b) -> p a b", b=2))
                cp("dve", Sgb[:, :, :, :], Sg[:, :, :, :])
                cp("dve", Slb[:, :, :, :], Sl[:, :, :, :])
                sample_kv_pass()
            for t in range(nt):
                src = xp[si, t * T:(t + 1) * T, :] if kind == "p" else xs[:, :]
                gi_ = tile_list.index((kind, si, t))
                load_tile_T(src, T, xT, q="sp", pre=(gi_ > 0))
                ckpt('load')
                hook = None
                if gi_ + 1 < len(tile_list):
                    k2, s2, t2 = tile_list[gi_ + 1]
                    T2 = TT if k2 == "p" else 64
                    src2 = xp[s2, t2 * T2:(t2 + 1) * T2, :] if k2 == "p" else xs[:, :]
                    hook = (lambda src2=src2, T2=T2: prefetch_tile(src2, T2))
                for l in range(L):
                    if cast_state["next"] == l + 1 and l + 1 < L:
                        cast_layer(l + 1)
                        cast_state["next"] = l + 2
                    layer(l, T, t == nt - 1, oslot, ffn_hook=(hook if l == L - 1 else None))
                m0 = ar_top[0]
                xo = aalloc([128, KC, T])
                rmsnorm(xT, PC_NFIN, T, xo)
                dst = yp[si, t * T:(t + 1) * T, :] if kind == "p" else ys[:, :]
                store_tile_T(xo, T, dst)
                ar_top[0] = m0
            for l in range(L):
                dma("pool", o_gdn[l, oslot].rearrange("h k v -> k h v"), Sg[:, l, :, :])
                for fc in range(2):
                    dma("pool", o_gla[l, oslot, 2 * fc:2 * fc + 2].rearrange("h d v -> (h d) v"), Sl[:, l, fc, :])

    try:
        ckpt('prologue')
        main_loop()
    except StopBuild:
        pass
    P.emit(nc, ES)
    ES.close()
    return nc, P


def _consts():
    c = np.zeros((128, 640), np.float32)
    i = np.arange(128)
    c[:, 0:128] = np.eye(128)
    c[:, 128:256] = (i[:, None] <= i[None, :])
    c[:, 256:384] = np.where(i[None, :] >= i[:, None], 0.0, -BIG)
    c[:, 384:512] = np.where(i[:, None] > i[None, :], 0.0, BIG)
    c[:, 512:640] = 1.0
    return c


def _prep_weights(inp, L):
    f = np.float32
    w_in = np.asarray(inp["w_in"], f)
    wst = np.zeros((L, NU, 128, UNIT), f)

    def unit_k8(W):
        return W.reshape(8, 128, 512).transpose(1, 0, 2).reshape(128, 4096)

    def unit_k4(W):
        o = np.zeros((128, 4096), f)
        o[:, :2048] = W.reshape(4, 128, 512).transpose(1, 0, 2).reshape(128, 2048)
        return o

    for l in range(L):
        wi = w_in[l]
        cols = [(0, 512), (512, 1024), (1024, 1536), (2056, 2568), (2568, 3080), (1544, 2056), (3096, 3608),
                (3608, 4120), (4120, 4632), (4632, 5144), (5144, 5656)]
        u = 0
        for a, b in cols:
            wst[l, u] = unit_k8(wi[:, a:b]); u += 1
        woa = np.asarray(inp["w_out_a"][l], f)
        wst[l, u] = unit_k4(woa[:, 0:512]); u += 1
        wst[l, u] = unit_k4(woa[:, 512:1024]); u += 1
        wob = np.asarray(inp["w_out_b"][l], f)
        wst[l, u] = unit_k4(wob[:, 0:512]); u += 1
        wst[l, u] = unit_k4(wob[:, 512:1024]); u += 1
        for name in ("w_o", "w_mq", "w_mo"):
            W = np.asarray(inp[name][l], f)
            wst[l, u] = unit_k8(W[:, 0:512]); u += 1
            wst[l, u] = unit_k8(W[:, 512:1024]); u += 1
        wu = np.asarray(inp["w_up"][l], f)
        for uu in range(11):
            W = np.concatenate([wu[:, 256 * uu:256 * uu + 256], wu[:, DFF + 256 * uu:DFF + 256 * uu + 256]], axis=1)
            wst[l, u] = unit_k8(W); u += 1
        wd = np.asarray(inp["w_down"][l], f)
        for half in range(2):
            for kg in range(3):
                nk = 8 if kg < 2 else 6
                o = np.zeros((128, 8, 512), f)
                o[:, :nk, :] = wd[kg * 1024:kg * 1024 + nk * 128, half * 512:(half + 1) * 512].reshape(nk, 128, 512).transpose(1, 0, 2)
                wst[l, u] = o.reshape(128, 4096); u += 1
        assert u == NU
    wkv = np.zeros((L, 4, 128, UNIT), f)
    for l in range(L):
        for wi_, name in enumerate(("w_mk", "w_mv")):
            W = np.asarray(inp[name][l], f)
            wkv[l, 2 * wi_] = unit_k8(W[:, 0:512])
            wkv[l, 2 * wi_ + 1] = unit_k8(W[:, 512:1024])
    wsm = np.zeros((L, 128, 8, 24), f)
    for l in range(L):
        wi = w_in[l]
        small = np.concatenate([wi[:, 3080:3096], wi[:, 1536:1544]], axis=1)
        wsm[l] = small.reshape(8, 128, 24).transpose(1, 0, 2)
    wsm = wsm.reshape(L, 128, 192)
    w2 = np.asarray(inp["w_gate_b2"], f)[:L].transpose(1, 0, 2).reshape(16, L * 256)
    pcol = np.zeros((128, L, NPC), f)

    def colmaj(v):
        return np.asarray(v, f).reshape(-1, 128).T

    for l in range(L):
        pcol[:, l, PC_NMIX:PC_NMIX + 8] = colmaj(inp["norm_mix"][l])
        pcol[:, l, PC_NMEM:PC_NMEM + 8] = colmaj(inp["norm_mem"][l])
        pcol[:, l, PC_NFFN:PC_NFFN + 8] = colmaj(inp["norm_ffn"][l])
        pcol[:, l, PC_NKV:PC_NKV + 8] = colmaj(inp["norm_memkv"][l])
        ca = np.asarray(inp["conv_a_w"][l], f)
        pcol[:, l, PC_CONVA:PC_CONVA + 48] = ca.reshape(4, 12, 128).transpose(2, 1, 0).reshape(128, 48)
        pcol[:, l, PC_ONA] = np.asarray(inp["onorm_a"][l], f)
        pcol[:, l, PC_ONB] = np.asarray(inp["onorm_b"][l], f)
        cf = np.asarray(inp["conv_f_w"][l], f)
        pcol[:, l, PC_CFW:PC_CFW + 132] = cf.reshape(3, 44, 128).transpose(2, 1, 0).reshape(128, 132)
        pcol[:, l, PC_CFB:PC_CFB + 44] = colmaj(inp["conv_f_b"][l])
        pcol[:, l, PC_NFIN:PC_NFIN + 8] = colmaj(inp["norm_final"])
    pcol = pcol.reshape(128, L * NPC)
    prow = np.zeros((L, NPR), f)
    for l in range(L):
        prow[l, 0:4] = inp["a_log"][l]
        prow[l, 4:8] = inp["dt_bias"][l]
        prow[l, 8:264] = inp["b_gate_b"][l]
    prow = prow.reshape(1, L * NPR)
    return dict(wst=wst, wkv=wkv, wsm=wsm, w2=np.ascontiguousarray(w2), pcol=np.ascontiguousarray(pcol), prow=prow, cst=_consts())


def _core_inputs(inp, shared, c, cfg):
    L, NP = cfg.L, cfg.NP
    f = np.float32
    m = dict(shared)
    m["xp"] = np.ascontiguousarray(np.asarray(inp["x_prompt"], f)[c * NP:(c + 1) * NP])
    m["memp"] = np.ascontiguousarray(np.asarray(inp["mem_prompt"], f)[c * NP:(c + 1) * NP])
    if cfg.sample:
        m["xs"] = np.ascontiguousarray(np.asarray(inp["x_sample"], f)[c])
        m["sgdn"] = np.ascontiguousarray(np.asarray(inp["state_gdn"], f)[:L, c])
        gc = np.asarray(inp["state_gdn_conv"], f)[:L, c]
        m["sgdc"] = np.ascontiguousarray(gc.reshape(L, 3, 12, 128).transpose(0, 3, 2, 1).reshape(L, 128, 36))
        m["sgla"] = np.ascontiguousarray(np.asarray(inp["state_gla"], f)[:L, c])
        fc = np.asarray(inp["state_ffn_conv"], f)[:L, c]
        m["sffc"] = np.ascontiguousarray(fc.reshape(L, 2, 44, 128).transpose(0, 3, 2, 1).reshape(L, 128, 88))
        m["cmk"] = np.ascontiguousarray(np.asarray(inp["cache_mem_k"], f)[:L, c].reshape(L, NMEM, D))
        m["cmv"] = np.ascontiguousarray(np.asarray(inp["cache_mem_v"], f)[:L, c].reshape(L, NMEM, D))
    return m


_CACHE = {}


def run(inp, cfg, ncores):
    key = (cfg.NP, cfg.SEQ, cfg.L, cfg.sample, cfg.TT, cfg.stop)
    if key not in _CACHE:
        _CACHE[key] = build(cfg)[0]
    nc = _CACHE[key]
    shared = _prep_weights(inp, cfg.L)
    in_maps = [_core_inputs(inp, shared, c, cfg) for c in range(ncores)]
    res = run_bass_kernel_spmd(nc, in_maps, core_ids=list(range(ncores)))
    return res.results


def kernel(**inp):
    cfg = Cfg()
    ncores = 8
    r = run(inp, cfg, ncores)
    L, NP = cfg.L, cfg.NP
    cat = np.concatenate
    y_prompt = cat([r[c]["yp"] for c in range(ncores)], axis=0)
    y_sample = np.stack([r[c]["ys"] for c in range(ncores)], axis=0)

    def split(name, shape_tail):
        p = cat([r[c][name][:, :NP] for c in range(ncores)], axis=1)
        s = cat([r[c][name][:, NP:NP + 1] for c in range(ncores)], axis=1)
        return p, s

    gdn_p, gdn_s = split("o_gdn", None)
    gdc_p, gdc_s = split("o_gdc", None)
    gla_p, gla_s = split("o_gla", None)
    ffc_p, ffc_s = split("o_ffc", None)
    mk = cat([r[c]["o_mk"] for c in range(ncores)], axis=1).reshape(L, 8 * NP, NMEM, 4, 256)
    mv = cat([r[c]["o_mv"] for c in range(ncores)], axis=1).reshape(L, 8 * NP, NMEM, 4, 256)
    outs = (y_prompt, y_sample, gdn_p, gdc_p, gla_p, ffc_p, mk, mv, gdn_s, gdc_s, gla_s, ffc_s)
    return tuple(np.ascontiguousarray(o, dtype=np.float32) for o in outs)
```
